# Optimizing a Trainium2 kernel written in Bass

```python
import math
import jax
import jax.numpy as jnp
from jax import lax
import numpy as np

D_MODEL = 1024
BATCH = 4
SEQ = 8192
DEPTH = 2
DEC_BATCH = 32
DEC_SEQ = 32
PAST_LEN = 1024

CHUNK = 64
N_PAST_CHUNKS = 8
ATT_HEADS = 8
ATT_HEAD_DIM = 64
ATT_WIDTH = ATT_HEADS * ATT_HEAD_DIM
MAX_REL = 256
REL_SIZE = MAX_REL + CHUNK
M_HEADS = 4
M_HEAD_DIM = 128
M_WIDTH = M_HEADS * M_HEAD_DIM
CONV_W = 4
D_FF = -(-8 * D_MODEL // (3 * 256)) * 256
IN_WIDTH = 3 * ATT_WIDTH + 4 * M_WIDTH + 2 * M_HEADS + 2 * D_MODEL
EPS = 1e-6
NEG = -1e30

kernel_name = 'hybrid_stream_bandattn_mlstm_step'


def _split_points():
    widths = [ATT_WIDTH] * 3 + [M_WIDTH] * 4 + [M_HEADS] * 2 + [D_MODEL] * 2
    return [int(p) for p in np.cumsum(widths)[:-1]]


def rmsnorm(x, g):
    xf = x.astype(jnp.float32)
    y = xf * lax.rsqrt(jnp.mean(xf * xf, axis=-1, keepdims=True) + EPS)
    return (y * g.astype(jnp.float32)).astype(x.dtype)


def head_rmsnorm(h, gain):
    B, T = h.shape[0], h.shape[1]
    y = h * lax.rsqrt(jnp.mean(h * h, axis=-1, keepdims=True) + EPS)
    return y.reshape(B, T, M_WIDTH) * gain.astype(jnp.float32)


def rel_bias_block(rel_bias, n_q, n_k, n_past):
    dist = jnp.arange(n_q)[:, None] + n_past - jnp.arange(n_k)[None, :]
    idx = jnp.clip(dist, -(CHUNK - 1), MAX_REL) + (CHUNK - 1)
    return rel_bias[:, idx]


def band_core(q, k, v, valid, bias):
    s = jnp.einsum('nqhd,nkhd->nhqk', q, k).astype(jnp.float32) * (ATT_HEAD_DIM ** -0.5)
    s = s + bias[None].astype(jnp.float32)
    s = jnp.where(valid[:, None, None, :], s, NEG)
    p = jax.nn.softmax(s, axis=-1).astype(v.dtype)
    return jnp.einsum('nhqk,nkhd->nqhd', p, v)


def prompt_band_attention(q, k, v, rel_bias):
    B, S, H, Dh = q.shape
    nc = S // CHUNK
    P = N_PAST_CHUNKS * CHUNK
    L = P + CHUNK
    idx = jnp.arange(nc)[:, None] * CHUNK + jnp.arange(L)[None, :]
    valid = idx >= P
    bias = rel_bias_block(rel_bias, CHUNK, L, P)

    def one(args):
        qs, ks, vs = args
        kp = jnp.pad(ks, ((P, 0), (0, 0), (0, 0)))[idx]
        vp = jnp.pad(vs, ((P, 0), (0, 0), (0, 0)))[idx]
        out = band_core(qs.reshape(nc, CHUNK, H, Dh), kp, vp, valid, bias)
        return out.reshape(S, H, Dh)

    return lax.map(one, (q, k, v))


def sample_band_attention(q, k, v, k_cache, v_cache, rel_bias):
    T = q.shape[1]
    pc = k_cache.shape[1]
    kb = jnp.concatenate([k_cache.astype(k.dtype), k], axis=1)[:, None]
    vb = jnp.concatenate([v_cache.astype(v.dtype), v], axis=1)[:, None]
    valid = jnp.ones((1, pc + T), dtype=bool)
    bias = rel_bias_block(rel_bias, T, pc + T, pc)
    out = jax.vmap(band_core, in_axes=(0, 0, 0, None, None))(q[:, None], kb, vb, valid, bias)
    return out[:, 0]


def mlstm_chunkwise(q, k, v, log_i, log_f, C0, n0, m0, block):
    B, T, H, Dk = q.shape
    Dv = v.shape[-1]
    nb = T // block

    def blocks(t):
        return t.astype(jnp.float32).reshape(B, nb, block, H, -1).transpose(1, 0, 3, 2, 4)

    def gblocks(t):
        return t.reshape(B, nb, block, H).transpose(1, 0, 3, 2)

    tril = jnp.tril(jnp.ones((block, block), dtype=bool))

    def step(carry, xs):
        C, n, m = carry
        qc, kc, vc, li, lf = xs
        b = jnp.cumsum(lf, axis=-1)
        a = b + m[..., None]
        D = b[..., :, None] - b[..., None, :] + li[..., None, :]
        D = jnp.where(tril, D, NEG)
        m_t = jnp.maximum(a, jnp.max(D, axis=-1))
        w_inter = jnp.exp(a - m_t)
        W = jnp.exp(D - m_t[..., None])
        S = jnp.einsum('bhtd,bhsd->bhts', qc, kc) * W
        num = jnp.einsum('bhts,bhsv->bhtv', S, vc) + w_inter[..., None] * jnp.einsum('bhtd,bhdv->bhtv', qc, C)
        den = jnp.sum(S, axis=-1) + w_inter * jnp.einsum('bhtd,bhd->bht', qc, n)
        h = num / jnp.maximum(jnp.abs(den), jnp.exp(-m_t))[..., None]
        m_new = m_t[..., -1]
        g_state = jnp.exp(b[..., -1] + m - m_new)
        w_s = jnp.exp(b[..., -1:] - b + li - m_new[..., None])
        C_new = g_state[..., None, None] * C + jnp.einsum('bhs,bhsd,bhsv->bhdv', w_s, kc, vc)
        n_new = g_state[..., None] * n + jnp.einsum('bhs,bhsd->bhd', w_s, kc)
        return (C_new, n_new, m_new), h

    carry0 = (C0.astype(jnp.float32), n0.astype(jnp.float32), m0.astype(jnp.float32))
    xs = (blocks(q), blocks(k), blocks(v), gblocks(log_i), gblocks(log_f))
    (C1, n1, m1), hs = lax.scan(step, carry0, xs)
    h = hs.transpose(1, 0, 3, 2, 4).reshape(B, T, H, Dv)
    return h, (C1, n1, m1)


def trunk_layer(x, c, k_cache, v_cache, conv_left, C0, n0, m0,
                w_ada, b_ada, g_mix, w_in, b_if, conv_w, conv_b, rel_bias, mh_gain,
                w_br_att, w_br_mlstm, w_out, g_ffn, w_gate_up, w_down):
    B, T, _ = x.shape
    dt = x.dtype
    mod = jax.nn.silu(c) @ w_ada + b_ada
    sh1, sc1, gt1, sh2, sc2, gt2 = jnp.split(mod[:, None, :], 6, axis=-1)

    h = rmsnorm(x, g_mix) * (1 + sc1) + sh1
    u = h @ w_in
    qa, ka, va, qm, km, vm, om, ip, fp, ga, gm = jnp.split(u, _split_points(), axis=-1)

    qa = qa.reshape(B, T, ATT_HEADS, ATT_HEAD_DIM)
    ka = ka.reshape(B, T, ATT_HEADS, ATT_HEAD_DIM)
    va = va.reshape(B, T, ATT_HEADS, ATT_HEAD_DIM)
    if k_cache is None:
        ya = prompt_band_attention(qa, ka, va, rel_bias)
        keep = min(N_PAST_CHUNKS * CHUNK, T)
        new_k, new_v = ka[:, T - keep:], va[:, T - keep:]
    else:
        ya = sample_band_attention(qa, ka, va, k_cache, v_cache, rel_bias)
        new_k, new_v = ka, va

    qk_in = jnp.concatenate([qm, km], axis=-1)
    xpad = jnp.concatenate([conv_left.astype(dt), qk_in], axis=1)
    conv = sum(xpad[:, j:j + T] * conv_w[j] for j in range(CONV_W)) + conv_b
    new_conv = xpad[:, -(CONV_W - 1):]
    conv = jax.nn.silu(conv)
    q_m, k_m = jnp.split(conv, 2, axis=-1)
    q_m = q_m.reshape(B, T, M_HEADS, M_HEAD_DIM)
    k_m = k_m.reshape(B, T, M_HEADS, M_HEAD_DIM) * (M_HEAD_DIM ** -0.5)
    v_m = vm.reshape(B, T, M_HEADS, M_HEAD_DIM)
    log_i = (ip + b_if[:M_HEADS]).astype(jnp.float32)
    log_f = jax.nn.log_sigmoid((fp + b_if[M_HEADS:]).astype(jnp.float32))
    block = CHUNK if T % CHUNK == 0 else T
    hm, (C1, n1, m1) = mlstm_chunkwise(q_m, k_m, v_m, log_i, log_f, C0, n0, m0, block)
    ym = (head_rmsnorm(hm, mh_gain) * jax.nn.sigmoid(om.astype(jnp.float32))).astype(dt)

    merged = (jax.nn.sigmoid(ga) * (ya.reshape(B, T, ATT_WIDTH) @ w_br_att)
              + jax.nn.sigmoid(gm) * (ym @ w_br_mlstm))
    x = x + gt1 * (merged @ w_out)

    h2 = rmsnorm(x, g_ffn) * (1 + sc2) + sh2
    g, up = jnp.split(h2 @ w_gate_up, 2, axis=-1)
    x = x + gt2 * ((jax.nn.silu(g) * up) @ w_down)
    return x, (new_k, new_v, new_conv, C1.astype(dt), n1.astype(dt), m1.astype(dt))


def setup_inputs(seed: int = 0) -> dict:
    key = jax.random.key(seed)
    ks = jax.random.split(key, 32)
    f32 = jnp.float32
    D = D_MODEL

    def nrm(k, shape, scale):
        return jax.random.normal(k, shape, f32) * scale

    pc = min(N_PAST_CHUNKS * CHUNK, PAST_LEN)
    b_if = jnp.concatenate([
        nrm(ks[14], (DEPTH, M_HEADS), 0.1),
        jnp.linspace(3.0, 6.0, M_HEADS, dtype=f32)[None, :] + nrm(ks[15], (DEPTH, M_HEADS), 0.1),
    ], axis=-1)
    return {
        'x_prompt': nrm(ks[0], (BATCH, SEQ, D), 1.0),
        'x_sample': nrm(ks[1], (DEC_BATCH, DEC_SEQ, D), 1.0),
        'cache_k': nrm(ks[2], (DEPTH, DEC_BATCH, pc, ATT_HEADS, ATT_HEAD_DIM), 1.0),
        'cache_v': nrm(ks[3], (DEPTH, DEC_BATCH, pc, ATT_HEADS, ATT_HEAD_DIM), 1.0),
        'state_conv': nrm(ks[4], (DEPTH, DEC_BATCH, CONV_W - 1, 2 * M_WIDTH), 1.0),
        'state_C': nrm(ks[5], (DEPTH, DEC_BATCH, M_HEADS, M_HEAD_DIM, M_HEAD_DIM), 1.0),
        'state_n': nrm(ks[6], (DEPTH, DEC_BATCH, M_HEADS, M_HEAD_DIM), 1.0),
        'state_m': nrm(ks[7], (DEPTH, DEC_BATCH, M_HEADS), 1.0),
        'c_prompt': nrm(ks[8], (BATCH, D), 1.0),
        'c_sample': nrm(ks[9], (DEC_BATCH, D), 1.0),
        'w_ada': nrm(ks[10], (DEPTH, D, 6 * D), 0.5 * D ** -0.5),
        'b_ada': nrm(ks[11], (DEPTH, 6 * D), 0.02),
        'g_mix': 1.0 + nrm(ks[12], (DEPTH, D), 0.02),
        'w_in': nrm(ks[13], (DEPTH, D, IN_WIDTH), D ** -0.5),
        'b_if': b_if,
        'conv_w': nrm(ks[16], (DEPTH, CONV_W, 2 * M_WIDTH), CONV_W ** -0.5),
        'conv_b': nrm(ks[17], (DEPTH, 2 * M_WIDTH), 0.02),
        'rel_bias': nrm(ks[18], (DEPTH, ATT_HEADS, REL_SIZE), 0.5),
        'mh_gain': 1.0 + nrm(ks[19], (DEPTH, M_WIDTH), 0.02),
        'w_br_att': nrm(ks[20], (DEPTH, ATT_WIDTH, D), ATT_WIDTH ** -0.5),
        'w_br_mlstm': nrm(ks[21], (DEPTH, M_WIDTH, D), M_WIDTH ** -0.5),
        'w_out': nrm(ks[22], (DEPTH, D, D), D ** -0.5),
        'g_ffn': 1.0 + nrm(ks[23], (DEPTH, D), 0.02),
        'w_gate_up': nrm(ks[24], (DEPTH, D, 2 * D_FF), D ** -0.5),
        'w_down': nrm(ks[25], (DEPTH, D_FF, D), D_FF ** -0.5),
        'g_final': 1.0 + nrm(ks[26], (D,), 0.02),
    }


def reference(x_prompt, x_sample, cache_k, cache_v, state_conv, state_C, state_n, state_m,
              c_prompt, c_sample, w_ada, b_ada, g_mix, w_in, b_if, conv_w, conv_b, rel_bias,
              mh_gain, w_br_att, w_br_mlstm, w_out, g_ffn, w_gate_up, w_down, g_final):
    bp = x_prompt.shape[0]
    xp, xs = x_prompt, x_sample
    outs_p, outs_s = [], []
    for l in range(DEPTH):
        wl = (w_ada[l], b_ada[l], g_mix[l], w_in[l], b_if[l], conv_w[l], conv_b[l], rel_bias[l],
              mh_gain[l], w_br_att[l], w_br_mlstm[l], w_out[l], g_ffn[l], w_gate_up[l], w_down[l])
        zc = jnp.zeros((bp, CONV_W - 1, 2 * M_WIDTH), xp.dtype)
        zC = jnp.zeros((bp, M_HEADS, M_HEAD_DIM, M_HEAD_DIM), jnp.float32)
        zn = jnp.zeros((bp, M_HEADS, M_HEAD_DIM), jnp.float32)
        zm = jnp.zeros((bp, M_HEADS), jnp.float32)
        xp, st_p = trunk_layer(xp, c_prompt, None, None, zc, zC, zn, zm, *wl)
        xs, st_s = trunk_layer(xs, c_sample, cache_k[l], cache_v[l], state_conv[l],
                               state_C[l], state_n[l], state_m[l], *wl)
        outs_p.append(st_p)
        outs_s.append(st_s)
    y_prompt = rmsnorm(xp, g_final)
    y_sample = rmsnorm(xs, g_final)

    def stk(outs, i):
        return jnp.stack([o[i] for o in outs], axis=0)

    return (y_prompt, y_sample,
            stk(outs_p, 0), stk(outs_p, 1), stk(outs_p, 2), stk(outs_p, 3), stk(outs_p, 4), stk(outs_p, 5),
            stk(outs_s, 0), stk(outs_s, 1), stk(outs_s, 2), stk(outs_s, 3), stk(outs_s, 4), stk(outs_s, 5))
```

```python
import contextlib
import numpy as np
import concourse.bass as bass
import concourse.mybir as mybir
from concourse.bass_utils import run_bass_kernel_spmd

F32 = mybir.dt.float32
BF16 = mybir.dt.bfloat16
AF = mybir.ActivationFunctionType
ALU = mybir.AluOpType
AX = mybir.AxisListType

D = 1024
KT = 8
NH = 8
DH = 64
MH = 4
MD = 128
DFF = 2816
FT = 22
INW = 5640
NT = 512
NSS = 4
TS = 32
EPS = 1e-6
NEGM = -30000.0


class Prog:
    ENGS = ("pe", "act", "dve", "pool", "sp")

    def __init__(self, nc):
        self.nc = nc
        self.ops = []
        self.deferred = []
        self.marks = []
        self.limit = None

    def op(self, eng, fn, reads=(), writes=(), dma_key=None, defer=False, wait_total=()):
        o = dict(eng=eng, fn=fn, reads=tuple(reads), writes=tuple(writes), dma_key=dma_key,
                 wait_total=tuple(wait_total))
        if defer:
            self.deferred.append(o)
        else:
            self.ops.append(o)

    def mark(self, name):
        self.flush()
        self.marks.append((name, len(self.ops)))

    def flush(self):
        self.ops.extend(self.deferred)
        self.deferred = []

    def build(self):
        self.flush()
        nc = self.nc
        if self.limit is not None:
            self.ops = self.ops[:self.limit]
        ops = self.ops

        def stream(o):
            return ("dma", o["dma_key"]) if o["dma_key"] is not None else o["eng"]

        last_w = {}
        readers = {}
        pos = []
        cnt = {}
        seen = {e: {} for e in self.ENGS}
        need = []
        by_sp = {}
        for i, o in enumerate(ops):
            s = stream(o)
            cnt[s] = cnt.get(s, 0) + 1
            pos.append(cnt[s])
            by_sp[(s, cnt[s])] = i
            deps = set()
            for r in o["reads"]:
                if r in last_w:
                    deps.add(last_w[r])
            for r in o["writes"]:
                if r in last_w:
                    deps.add(last_w[r])
                deps.update(readers.get(r, ()))
            deps.discard(i)
            for r in o["reads"]:
                readers.setdefault(r, []).append(i)
            for r in o["writes"]:
                last_w[r] = i
                readers[r] = []
            w = {}
            for d in deps:
                sd = stream(ops[d])
                if sd == "pe" and o["eng"] == "pe" and o["dma_key"] is None:
                    continue
                if pos[d] > w.get(sd, 0):
                    w[sd] = pos[d]
            lst = []
            sn = seen[o["eng"]]
            for sd, p in w.items():
                if sn.get(sd, 0) >= p:
                    continue
                sn[sd] = p
                lst.append((sd, p))
            need.append(lst)
        marked = set()
        for lst in need:
            for sd, p in lst:
                if not isinstance(sd, tuple):
                    marked.add(by_sp[(sd, p)])
        val = {}
        run = {}
        wtot = {}
        for i, o in enumerate(ops):
            s = stream(o)
            lst = []
            for k in o["wait_total"]:
                sk = ("dma", k)
                v = run.get(sk, 0)
                if v > seen[o["eng"]].get(("tot", k), 0):
                    seen[o["eng"]][("tot", k)] = v
                    lst.append((sk, v))
            wtot[i] = lst
            if o["dma_key"] is not None:
                run[s] = run.get(s, 0) + 16
                val[i] = run[s]
            elif i in marked:
                run[s] = run.get(s, 0) + 1
                val[i] = run[s]
        streams = sorted(set(stream(o) for o in ops), key=str)
        self.n_ops = len(ops)
        with contextlib.ExitStack() as es:
            sems = {}
            for k, s in enumerate(streams):
                sems[s] = es.enter_context(nc.semaphore("sem%d" % k))
            block = es.enter_context(nc.Block())
            hooks = {"pe": block.tensor, "act": block.scalar, "dve": block.vector,
                     "pool": block.gpsimd, "sp": block.sync}
            for e in self.ENGS:
                idxs = [i for i, o in enumerate(ops) if o["eng"] == e]

                def body(eng, idxs=idxs, e=e):
                    for i in idxs:
                        o = ops[i]
                        for sd, p in need[i]:
                            eng.wait_ge(sems[sd], val[by_sp[(sd, p)]])
                        for sk, v in wtot[i]:
                            eng.wait_ge(sems[sk], v)
                        ins = o["fn"](eng)
                        if o["dma_key"] is not None:
                            ins.then_inc(sems[stream(o)], 16)
                        elif i in marked:
                            ins.then_inc(sems[stream(o)], 1)
                    if e == "sp":
                        for s, v in run.items():
                            if isinstance(s, tuple):
                                eng.wait_ge(sems[s], v)
                hooks[e](body)


def build_program(SEQ):
    assert SEQ % NT == 0
    NTILES = SEQ // NT
    nc = bass.Bass("TRN2", target_bir_lowering=False)
    P = Prog(nc)

    def din(name, shape):
        return nc.dram_tensor(name, list(shape), F32, kind="ExternalInput")

    def dout(name, shape):
        return nc.dram_tensor(name, list(shape), F32, kind="ExternalOutput")

    xp = din("xp", [SEQ, D])
    xs = din("xs", [NSS * TS, D])
    c5 = din("c5", [1 + NSS, D])
    ck = din("ck", [2, NSS, 512, 512])
    cv = din("cv", [2, NSS, 512, 512])
    sconv = din("sconv", [2, NSS, 24, 128])
    sC = din("sC", [2, NSS, MH, MD, MD])
    sn = din("sn", [2, NSS, MH, MD])
    sm = din("sm", [2, NSS * MH])
    vecs = din("vecs", [2, 116, 128])
    b_if = din("b_if", [2, 8])
    rel_bias = din("rel_bias", [2, NH, 320])
    w_ada = din("w_ada", [2, D, 6 * D])
    w_in = din("w_in", [2, D, INW])
    w_bra = din("w_bra", [2, 512, D])
    w_brm = din("w_brm", [2, 512, D])
    w_out = din("w_out", [2, D, D])
    w_gu = din("w_gu", [2, D, 2 * DFF])
    w_dn = din("w_dn", [2, DFF, D])

    yp = dout("yp", [SEQ, D])
    ys = dout("ys", [NSS * TS, D])
    nkp = dout("nkp", [2, 512, 512])
    nvp = dout("nvp", [2, 512, 512])
    ncp = dout("ncp", [2, 24, 128])
    nCp = dout("nCp", [2, MH, MD, MD])
    nnp = dout("nnp", [2, MH, MD])
    nmp = dout("nmp", [2, MH])
    nks = dout("nks", [2, NSS, TS, 512])
    nvs = dout("nvs", [2, NSS, TS, 512])
    ncs = dout("ncs", [2, NSS, 24, 128])
    nCs = dout("nCs", [2, NSS, MH, MD, MD])
    nns = dout("nns", [2, NSS, MH, MD])
    nms = dout("nms", [2, NSS, MH])

    SL = 4096
    wsc = {
        "in": nc.dram_tensor("wsc_in", [2, 12, 128, SL], BF16),
        "bra": nc.dram_tensor("wsc_bra", [2, 2, 128, SL], BF16),
        "brm": nc.dram_tensor("wsc_brm", [2, 2, 128, SL], BF16),
        "out": nc.dram_tensor("wsc_out", [2, 2, 128, SL], BF16),
        "gu": nc.dram_tensor("wsc_gu", [2, FT, 128, SL], BF16),
        "dn": nc.dram_tensor("wsc_dn", [2, 8, 128, SL], BF16),
    }
    xscr = nc.dram_tensor("xscr", [KT, 128, SEQ], F32)
    Escr = nc.dram_tensor("Escr", [2, NH, 512], F32)

    def sb(name, shape, dt=F32):
        return nc.alloc_sbuf_tensor(name, list(shape), dt)

    idb = sb("idb", [128, 128], BF16)
    idf = sb("idf", [128, 128])
    Jf = sb("Jf", [128, 128])
    ones_bf = sb("ones_bf", [128, 128], BF16)
    tri = sb("tri", [128, 128], BF16)
    vecT = sb("vecT", [128, 2, 116])
    modT = sb("modT", [128, 2, 48, 5])
    gsc1 = sb("gsc1", [128, 8, 5])
    gsc2 = sb("gsc2", [128, 8, 5])
    bif = sb("bif", [128, 8])
    scT = sb("scT", [128, 8, 5])
    dg = sb("dg", [8, 8])
    ones8 = sb("ones8", [8, 128])
    chead = sb("chead", [128, 8])

    xT = sb("xT", [128, KT, NT])
    xsT = sb("xsT", [128, KT, 128])
    xin = sb("xin", [128, D])
    hT = sb("hT", [128, KT, NT], BF16)
    sq = sb("sq", [128, 2, NT], BF16)
    rstd = sb("rstd", [128, NT])
    ftmp = sb("ftmp", [128, 3, NT])
    wring = sb("wring", [128, 3, SL], BF16)
    qaT = sb("qaT", [128, 4, NT], BF16)
    kaT = sb("kaT", [128, 4, 2 * NT], BF16)
    Vring = sb("Vring", [128, 8, NH, DH + 1], BF16)
    cst = sb("cst", [128, 1, NT + 4])
    ctail = sb("ctail", [128, 3, 8])
    ctail_s = sb("ctail_s", [128, 3, 8, NSS])
    qmT = sb("qmT", [128, 4, NT], BF16)
    ksilu = sb("ksilu", [128, 4, NT], BF16)
    omg = sb("omg", [128, 4, NT], BF16)
    vaug = sb("vaug", [128, 4, MH, MD + 1], BF16)
    gate_tm = sb("gate_tm", [128, 4, 8])
    btab = sb("btab", [128, 3, NH, 128])
    PT = sb("PT", [128, 5, NH, 128], BF16)
    rden = sb("rden", [128, NH])
    ya = sb("ya", [128, NH, DH], BF16)
    yaT = sb("yaT", [128, 4, NT], BF16)
    gli = sb("gli", [128, 4])
    gsp = sb("gsp", [128, 4])
    ghl = sb("ghl", [128, 4, 4], BF16)
    gres = sb("gres", [128, 4])
    reps = sb("reps", [128, 4, MH, 128], BF16)
    Rm = sb("Rm", [128, 4])
    cm = sb("cm", [128, 4])
    mtmp = sb("mtmp", [128, 4])
    a_bc = sb("a_bc", [128, MH, 128])
    clampv = sb("clampv", [128, MH, 128])
    kpT = sb("kpT", [128, MH, 128], BF16)
    ktm = sb("ktm", [128, MH, 128], BF16)
    Sc_bf = sb("Sc_bf", [128, MH, MD + 1], BF16)
    nrep = sb("nrep", [128, MH, 128], BF16)
    STm = sb("STm", [128, MH, 128], BF16)
    t1 = sb("t1", [128, MH, 128])
    t2 = sb("t2", [128, MH, 128])
    sqh = sb("sqh", [128, MH, 128], BF16)
    state_p = sb("state_p", [128, MH, MD + 1])
    m_p = sb("m_p", [128, 4])
    state_s = sb("state_s", [128, MH, MD + 1])
    m_s = sb("m_s", [128, 4])
    ymT = sb("ymT", [128, 4, NT], BF16)
    actT = sb("actT", [128, FT, NT], BF16)
    ckb = actT[:, 0:4, :]
    ckT = actT[:, 4:8, :]
    cvb = actT[:, 8:13, :].rearrange("p a b -> p (a b)")[:, 0:4 * NH * 65].rearrange(
        "p (k h d) -> p k h d", k=4, h=NH)

    c5sb = xin[0:8, :]
    vrow = xin[:, 0:128]
    Esb = xin[0:8, 0:512]
    stmp = xin[:, :].rearrange("p (h q) -> p h q", h=NH)
    hm = a_bc
    ones_f = ftmp[:, 0, 0:128]
    tri_f = ftmp[:, 1, 0:128]

    def mstash(ct):
        return (qaT[:, ct, :], "qaT") if ct < 4 else (qmT[:, ct - 4, :], "qmT")

    def mergedv(ct):
        return (ksilu[:, ct, :], "ksilu") if ct < 4 else (omg[:, ct - 4, :], "omg")

    psf = nc.alloc_psum_tensor("psf", [128, 7, 512], F32)
    psb = nc.alloc_psum_tensor("psb", [128, 2, 512], BF16)

    def fb(b):
        return psf[:, b, :]

    def mm(out, lhsT, rhs, start, stop, rd, wr):
        P.op("pe", lambda e: e.matmul(out, lhsT=lhsT, rhs=rhs, start=start, stop=stop), rd, wr)

    def tr(out, in_, ident, rd, wr):
        P.op("pe", lambda e: e.transpose(out=out, in_=in_, identity=ident), rd, wr)

    def act(out, in_, func, rd, wr, bias=0.0, scale=1.0):
        P.op("act", lambda e: e.activation(out=out, in_=in_, func=func, bias=bias, scale=scale), rd, wr)

    def cp(eng, out, in_, rd, wr):
        if eng == "act":
            P.op("act", lambda e: e.activation(out=out, in_=in_, func=AF.Identity), rd, wr)
        else:
            P.op(eng, lambda e: e.tensor_copy(out=out, in_=in_), rd, wr)

    def tt(eng, out, in0, in1, op, rd, wr):
        P.op(eng, lambda e: e.tensor_tensor(out=out, in0=in0, in1=in1, op=op), rd, wr)

    def ts(eng, out, in0, s1, s2, op0, op1, rd, wr):
        if s2 is None:
            P.op(eng, lambda e: e.tensor_scalar(out=out, in0=in0, scalar1=s1, scalar2=None, op0=op0), rd, wr)
        else:
            P.op(eng, lambda e: e.tensor_scalar(out=out, in0=in0, scalar1=s1, scalar2=s2, op0=op0, op1=op1), rd, wr)

    def stt(eng, out, in0, scalar, in1, op0, op1, rd, wr):
        P.op(eng, lambda e: e.scalar_tensor_tensor(out=out, in0=in0, scalar=scalar, in1=in1, op0=op0, op1=op1), rd, wr)

    def recip(out, in_, rd, wr):
        P.op("dve", lambda e: e.reciprocal(out=out, in_=in_), rd, wr)

    def memset(eng, ap, v, wr):
        P.op(eng, lambda e: e.memset(ap, v), (), wr)

    def dma(out, in_, rd, wr, key, eng="sp", defer=False, wait_total=()):
        P.op(eng, lambda e: e.dma_start(out=out, in_=in_), rd, wr, dma_key=key, defer=defer, wait_total=wait_total)

    rr = {"ft": 0, "acc": 0, "w": 0}

    def next_ftmp():
        i = rr["ft"] % 3
        rr["ft"] += 1
        return ftmp[:, i, :], "ftmp%d" % i

    def next_acc():
        i = rr["acc"] % 4
        rr["acc"] += 1
        return i, "f%d" % i

    memset("pool", idf[:], 0.0, ["idf"])
    P.op("pool", lambda e: e.affine_select(out=idf[:], in_=idf[:], pattern=[[-1, 128]], compare_op=ALU.not_equal,
                                           fill=1.0, base=0, channel_multiplier=1), ["idf"], ["idf"])
    memset("pool", Jf[:], 0.0, ["Jf"])
    P.op("pool", lambda e: e.affine_select(out=Jf[:], in_=Jf[:], pattern=[[1, 128]], compare_op=ALU.not_equal,
                                           fill=1.0, base=-127, channel_multiplier=1), ["Jf"], ["Jf"])
    memset("pool", tri_f, 1.0, ["ftmp1"])
    P.op("pool", lambda e: e.affine_select(out=tri_f, in_=tri_f, pattern=[[1, 128]], compare_op=ALU.is_ge,
                                           fill=0.0, base=0, channel_multiplier=-1), ["ftmp1"], ["ftmp1"])
    memset("pool", ones_f, 1.0, ["ftmp0"])
    memset("pool", ones8[:, :], 1.0, ["ones8"])
    cp("dve", idb[:], idf[:], ["idf"], ["idb"])
    cp("dve", tri[:], tri_f, ["ftmp1"], ["tri"])
    cp("dve", ones_bf[:], ones_f, ["ftmp0"], ["ones_bf"])
    memset("pool", Vring[:, :, :, DH:DH + 1], 1.0, ["Vring"])
    memset("pool", vaug[:, :, :, MD:MD + 1], 1.0, ["vaug"])
    memset("pool", cvb[:, :, :, DH:DH + 1], 1.0, ["actT"])

    def conv_panel(dst, src_cols, kt, pw, key):
        src = src_cols.rearrange("(kt p) c -> p kt c", p=128)
        d = dst[:, 0:kt * pw].rearrange("p (k c) -> p k c", k=kt)
        dma(d, src, [], [], key, eng="pool")

    IN_PANELS = [(512 * i, 512) for i in range(7)] + [(3584, 8), (3592, 512), (4104, 512), (4616, 512), (5128, 512)]

    def convert_layer(l):
        for pi, (c0, w) in enumerate(IN_PANELS):
            conv_panel(wsc["in"].ap()[l, pi], w_in.ap()[l][:, c0:c0 + w], 8, w, "cv%d_in" % l)
        for pi in range(2):
            conv_panel(wsc["bra"].ap()[l, pi], w_bra.ap()[l][:, pi * 512:(pi + 1) * 512], 4, 512, "cv%d_bra" % l)
            conv_panel(wsc["brm"].ap()[l, pi], w_brm.ap()[l][:, pi * 512:(pi + 1) * 512], 4, 512, "cv%d_brm" % l)
            conv_panel(wsc["out"].ap()[l, pi], w_out.ap()[l][:, pi * 512:(pi + 1) * 512], 8, 512, "cv%d_out" % l)
        for f in range(FT):
            key = "cv%d_gu" % l
            dstp = wsc["gu"].ap()[l, f][:, 0:8 * 256].rearrange("p (k c) -> p k c", k=8)
            srcg = w_gu.ap()[l][:, f * 128:(f + 1) * 128].rearrange("(kt p) c -> p kt c", p=128)
            srcu = w_gu.ap()[l][:, DFF + f * 128:DFF + (f + 1) * 128].rearrange("(kt p) c -> p kt c", p=128)
            dma(dstp[:, :, 0:128], srcg, [], [], key, eng="pool")
            dma(dstp[:, :, 128:256], srcu, [], [], key, eng="pool")
        for c in range(8):
            conv_panel(wsc["dn"].ap()[l, c], w_dn.ap()[l][:, c * 128:(c + 1) * 128], FT, 128, "cv%d_dn" % l)

    convert_layer(0)
    convert_layer(1)

    def load_panel(which, l, pi, kt, pw):
        s = rr["w"] % 3
        rr["w"] += 1
        dma(wring[:, s, 0:kt * pw], wsc[which].ap()[l, pi][:, 0:kt * pw], [], ["w%d" % s], "wl%d" % s,
            wait_total=["cv%d_%s" % (l, which)])
        return wring[:, s, 0:kt * pw].rearrange("p (k c) -> p k c", k=kt), "w%d" % s

    for l in range(2):
        dma(vrow[0:116, :], vecs.ap()[l], [], ["xin"], "xin")
        tr(fb(0)[:, 0:116], vrow[0:116, :], idf[0:116, 0:116], ["xin", "idf"], ["f0"])
        cp("dve", vecT[:, l, :], fb(0)[:, 0:116], ["f0"], ["vecT"])
    dma(c5sb[0:5, :], c5.ap(), [], ["xin"], "c5")
    act(c5sb[0:5, :], c5sb[0:5, :], AF.Silu, ["xin"], ["xin"])
    for kt in range(KT):
        tr(fb(1)[:, kt * 8:kt * 8 + 5], c5sb[0:5, kt * 128:(kt + 1) * 128], idf[0:5, 0:5], ["xin", "idf"], ["f1"])
    cp("dve", scT[:], fb(1)[:, 0:64].rearrange("p (k r) -> p k r", r=8)[:, :, 0:5], ["f1"], ["scT"])
    for l in range(2):
        for pi in range(24):
            s = pi % 2
            wa = xT[:, :, s * 256:(s + 1) * 256]
            dma(wa, w_ada.ap()[l][:, pi * 256:(pi + 1) * 256].rearrange("(kt p) c -> p kt c", p=128),
                [], ["xTa%d" % s, "xT"], "wada%d" % s)
            for c in range(2):
                t = pi * 2 + c
                bi, bn = next_acc()
                for kt in range(KT):
                    mm(fb(bi)[:, 0:5], wa[:, kt, c * 128:(c + 1) * 128], scT[:, kt, :], kt == 0, kt == KT - 1,
                       ["xTa%d" % s, "scT"], [bn])
                ts("dve", modT[:, l, t, :], fb(bi)[:, 0:5], vecT[:, l, t:t + 1], None, ALU.add, None,
                   [bn, "vecT"], ["modT"])

    def rms_stats(xv, N, invd):
        nk = len(xv)
        for kt in range(nk):
            s = kt % 2
            act(sq[:, s, 0:N], xv[kt][0], AF.Square, xv[kt][1], ["sq%d" % s])
            mm(psf[:, 4, 0:N], ones_bf[:], sq[:, s, 0:N], kt == 0, kt == nk - 1, ["sq%d" % s, "ones_bf"], ["f4"])
        act(rstd[:, 0:N], psf[:, 4, 0:N], AF.Sqrt, ["f4"], ["rstd"], bias=EPS, scale=invd)
        recip(rstd[:, 0:N], rstd[:, 0:N], ["rstd"], ["rstd"])

    def norm_mod(xbuf, xname, N, segs, gsc, shbase, l):
        rms_stats([(xbuf[:, kt, 0:N], [xname]) for kt in range(KT)], N, 1.0 / D)
        for kt in range(KT):
            ft_, fn_ = next_ftmp()
            for (c0, n, r) in segs:
                stt("dve", ft_[:, c0:c0 + n], xbuf[:, kt, c0:c0 + n], gsc[:, kt, r:r + 1], rstd[:, c0:c0 + n],
                    ALU.mult, ALU.mult, [xname, "rstd", "gsc"], [fn_])
            for (c0, n, r) in segs:
                act(hT[:, kt, c0:c0 + n], ft_[:, c0:c0 + n], AF.Identity, [fn_, "modT"], ["hT"],
                    bias=modT[:, l, shbase + kt, r:r + 1], scale=1.0)

    def fm_proj(panel, pname, kt_n, ct, rhs_buf, rhs_name, N):
        bi, bn = next_acc()
        for kt in range(kt_n):
            if callable(rhs_buf):
                rv, rn = rhs_buf(kt)
                rv = rv[:, 0:N]
            else:
                rv, rn = rhs_buf[:, kt, 0:N], rhs_name
            mm(fb(bi)[:, 0:N], panel[:, kt, ct * 128:(ct + 1) * 128], rv, kt == 0, kt == kt_n - 1,
               [pname, rn], [bn])
        return fb(bi)[:, 0:N], bn

    def tm_proj(panel, pname, g0, gn, ncols):
        bi, bn = next_acc()
        for kt in range(KT):
            mm(psf[0:gn, bi, 0:ncols], hT[:, kt, g0:g0 + gn], panel[:, kt, 0:ncols], kt == 0, kt == KT - 1,
               [pname, "hT"], [bn])
        return psf[0:gn, bi, 0:ncols], bn

    def mlstm_chunk(T, c0, blk, state, sname, mst, mname):
        npart = T
        g = gate_tm[0:npart, blk, :]
        tt("dve", gli[0:npart, :], g[:, 0:4], bif[0:npart, 0:4], ALU.add, ["gate_tm", "bif"], ["gli"])
        tt("dve", gsp[0:npart, :], g[:, 4:8], bif[0:npart, 4:8], ALU.add, ["gate_tm", "bif"], ["gsp"])
        act(gsp[0:npart, :], gsp[0:npart, :], AF.Exp, ["gsp"], ["gsp"], scale=-1.0)
        act(gsp[0:npart, :], gsp[0:npart, :], AF.Ln, ["gsp"], ["gsp"], bias=1.0)
        cp("dve", ghl[0:npart, 0, :], gli[0:npart, :], ["gli"], ["ghl"])
        tt("dve", gres[0:npart, :], gli[0:npart, :], ghl[0:npart, 0, :], ALU.subtract, ["gli", "ghl"], ["gres"])
        cp("dve", ghl[0:npart, 1, :], gres[0:npart, :], ["gres"], ["ghl"])
        cp("dve", ghl[0:npart, 2, :], gsp[0:npart, :], ["gsp"], ["ghl"])
        tt("dve", gres[0:npart, :], gsp[0:npart, :], ghl[0:npart, 2, :], ALU.subtract, ["gsp", "ghl"], ["gres"])
        cp("dve", ghl[0:npart, 3, :], gres[0:npart, :], ["gres"], ["ghl"])
        for w in range(4):
            cp("pool", reps[0:npart, w, :, :], ghl[0:npart, w, :].unsqueeze(2).to_broadcast([npart, 4, 128]),
               ["ghl"], ["reps"])
        G = psf[:, 0, :].rearrange("p (h t) -> p h t", h=MH)[:, :, 0:T]
        Bp = psf[:, 1, :].rearrange("p (h t) -> p h t", h=MH)[:, :, 0:T]
        for h in range(MH):
            mm(G[:, h, :], reps[0:npart, 0, h, :], idb[0:npart, 0:T], True, False, ["reps", "idb"], ["f0"])
            mm(G[:, h, :], reps[0:npart, 1, h, :], idb[0:npart, 0:T], False, False, ["reps", "idb"], ["f0"])
            mm(G[:, h, :], reps[0:npart, 2, h, :], tri[0:npart, 0:T], False, False, ["reps", "tri"], ["f0"])
            mm(G[:, h, :], reps[0:npart, 3, h, :], tri[0:npart, 0:T], False, True, ["reps", "tri"], ["f0"])
            mm(Bp[:, h, :], reps[0:npart, 2, h, :], tri[0:npart, 0:T], True, False, ["reps", "tri"], ["f1"])
            mm(Bp[:, h, :], reps[0:npart, 3, h, :], tri[0:npart, 0:T], False, True, ["reps", "tri"], ["f1"])
        P.op("dve", lambda e: e.tensor_reduce(out=Rm[:, :], in_=G, axis=AX.X, op=ALU.max), ["f0"], ["Rm"])
        tt("dve", Rm[:, :], Rm[:, :], mst[:, :], ALU.max, ["Rm", mname], ["Rm"])
        Rb = Rm[:, :].unsqueeze(2).to_broadcast([128, MH, T])
        tt("dve", t1[:, :, 0:T], G, Rb, ALU.subtract, ["f0", "Rm"], ["t1"])
        act(a_bc[:, :, 0:T], t1[:, :, 0:T], AF.Exp, ["t1"], ["a_bc"])
        tt("dve", mtmp[:, :], mst[:, :], Rm[:, :], ALU.subtract, [mname, "Rm"], ["mtmp"])
        act(cm[:, :], mtmp[:, :], AF.Exp, ["mtmp"], ["cm"])
        tt("dve", t1[:, :, 0:T], Bp, Rb, ALU.subtract, ["f1", "Rm"], ["t1"])
        act(clampv[:, :, 0:T], t1[:, :, 0:T], AF.Exp, ["t1"], ["clampv"])
        tt("dve", mst[:, :], Rm[:, :], Bp[:, :, T - 1], ALU.subtract, ["Rm", "f1", "mtmp"], [mname])
        stt("dve", kpT[:, :, 0:T], ksilu[:, :, c0:c0 + T], float(MD ** -0.5), a_bc[:, :, 0:T], ALU.mult, ALU.mult,
            ["ksilu", "a_bc"], ["kpT"])
        for h in range(MH):
            tr(psb[0:T, 0, h * 128:(h + 1) * 128], kpT[:, h, 0:T], idb[:, :], ["kpT", "idb"], ["tb0"])
        cp("act", ktm[0:npart, :, :], psb[0:T, 0, :].rearrange("p (h d) -> p h d", h=MH), ["tb0"], ["ktm"])
        tt("dve", Sc_bf[:, :, :], state[:, :, :], cm[:, :].unsqueeze(2).to_broadcast([128, MH, MD + 1]), ALU.mult,
           [sname, "cm"], ["Sc_bf"])
        cp("pool", nrep[:, :, :], Sc_bf[:, :, MD:MD + 1].to_broadcast([128, MH, 128]), ["Sc_bf"], ["nrep"])
        STp = psf[:, 2, :].rearrange("p (h t) -> p h t", h=MH)[0:T, :, 0:T]
        for h in range(MH):
            mm(STp[:, h, :], kpT[:, h, 0:T], qmT[:, h, c0:c0 + T], True, True, ["kpT", "qmT"], ["f2"])
        tt("dve", STm[0:npart, :, 0:T], STp, tri[0:T, 0:T].unsqueeze(1).to_broadcast([T, MH, T]), ALU.mult,
           ["f2", "tri"], ["STm"])
        NTp = psf[:, 0, :].rearrange("p (h t) -> p h t", h=MH)[:, :, 0:T]
        DTp = psf[:, 1, :].rearrange("p (h t) -> p h t", h=MH)[:, :, 0:T]
        for h in range(MH):
            mm(NTp[:, h, :], vaug[0:npart, blk, h, 0:MD], STm[0:npart, h, 0:T], True, False, ["vaug", "STm"], ["f0"])
            mm(NTp[:, h, :], Sc_bf[:, h, 0:MD], qmT[:, h, c0:c0 + T], False, True, ["Sc_bf", "qmT"], ["f0"])
            mm(DTp[:, h, :], ones_bf[0:npart, :], STm[0:npart, h, 0:T], True, False, ["ones_bf", "STm"], ["f1"])
            mm(DTp[:, h, :], nrep[:, h, :], qmT[:, h, c0:c0 + T], False, True, ["nrep", "qmT"], ["f1"])
        tt("dve", t1[:, :, 0:T], DTp, clampv[:, :, 0:T], ALU.max, ["f1", "clampv"], ["t1"])
        stt("dve", t2[:, :, 0:T], DTp, -1.0, t1[:, :, 0:T], ALU.mult, ALU.max, ["f1", "t1"], ["t2"])
        recip(t2[:, :, 0:T], t2[:, :, 0:T], ["t2"], ["t2"])
        tt("dve", hm[:, :, 0:T], NTp, t2[:, :, 0:T], ALU.mult, ["f0", "t2"], ["a_bc"])
        act(sqh[:, :, 0:T], hm[:, :, 0:T], AF.Square, ["a_bc"], ["sqh"])
        HS = psf[:, 2, :].rearrange("p (h t) -> p h t", h=MH)[:, :, 0:T]
        for h in range(MH):
            mm(HS[:, h, :], ones_bf[:, :], sqh[:, h, 0:T], True, True, ["ones_bf", "sqh"], ["f2"])
        act(t1[:, :, 0:T], HS, AF.Sqrt, ["f2"], ["t1"], bias=EPS, scale=1.0 / MD)
        recip(t1[:, :, 0:T], t1[:, :, 0:T], ["t1"], ["t1"])
        tt("dve", t2[:, :, 0:T], hm[:, :, 0:T], t1[:, :, 0:T], ALU.mult, ["a_bc", "t1"], ["t2"])
        tt("dve", ymT[:, :, c0:c0 + T], t2[:, :, 0:T], omg[:, :, c0:c0 + T], ALU.mult, ["t2", "omg"], ["ymT"])
        for h in range(MH):
            bank = 0 if h < 2 else 1
            o = psf[:, bank, (h % 2) * 129:(h % 2) * 129 + 129]
            mm(o, ktm[0:npart, h, :], vaug[0:npart, blk, h, :], True, True, ["ktm", "vaug"], ["f%d" % bank])
        for h in range(MH):
            bank = 0 if h < 2 else 1
            o = psf[:, bank, (h % 2) * 129:(h % 2) * 129 + 129]
            stt("dve", state[:, h, :], state[:, h, :], cm[:, h:h + 1], o, ALU.mult, ALU.add,
                [sname, "cm", "f%d" % bank, "Sc_bf"], [sname])

    S2 = psf[:, 4:6, :].rearrange("p a (h q) -> p (a h) q", q=128)
    S4 = psf[:, 4:6, :].rearrange("p a (s q) -> p a s q", q=128)

    def hview(ap3):
        return ap3.rearrange("p (s a) q -> p a s q", a=2)

    def attn_scores(kb, lhs_fn, nk, q0, nq, table, trd):
        for h in range(NH):
            lhsT, lrd = lhs_fn(h)
            mm(S4[0:nk, h % 2, h // 2, 0:nq], lhsT, qaT[(h % 2) * 64:(h % 2) * 64 + 64, h // 2, q0:q0 + nq], True, True,
               lrd + ["qaT"], ["f%d" % (4 + h % 2)])
        tt("dve", hview(stmp[0:nk, :, 0:nq]), S4[0:nk, :, :, 0:nq], table, ALU.add, ["f4", "f5"] + trd, ["xin"])
        act(PT[0:nk, kb, :, 0:nq], stmp[0:nk, :, 0:nq], AF.Exp, ["xin"], ["PT"])

    OA = psf[:, 6, 0:260].rearrange("p (h d) -> p h d", d=65)
    OB = psf[:, 3, 0:260].rearrange("p (h d) -> p h d", d=65)

    def attn_out(nq, pv_list, q0):
        for h in range(NH):
            O = OA if h < 4 else OB
            bn = "f6" if h < 4 else "f3"
            for i, (kb, nk, rhs_fn) in enumerate(pv_list):
                rhs, rrd = rhs_fn(h)
                mm(O[0:nq, h % 4, :], PT[0:nk, kb, h, 0:nq], rhs, i == 0, i == len(pv_list) - 1, ["PT"] + rrd, [bn])
        for half, (O, bn) in enumerate(((OA, "f6"), (OB, "f3"))):
            recip(rden[0:nq, half * 4:half * 4 + 4], O[0:nq, :, DH], [bn], ["rden"])
            tt("dve", ya[0:nq, half * 4:half * 4 + 4, :], O[0:nq, :, 0:DH],
               rden[0:nq, half * 4:half * 4 + 4].unsqueeze(2).to_broadcast([nq, 4, DH]), ALU.mult,
               [bn, "rden"], ["ya"])
        yaf = ya[:, :, :].rearrange("p h d -> p (h d)")
        for f in range(4):
            tr(psb[:, 1, f * 128:f * 128 + nq], yaf[0:nq, f * 128:(f + 1) * 128], idb[0:nq, 0:nq], ["ya", "idb"], ["tb1"])
        cp("act", yaT[:, :, q0:q0 + nq], psb[:, 1, :].rearrange("p (f q) -> p f q", f=4)[:, :, 0:nq], ["tb1"], ["yaT"])

    def conv_evac(ps, bn, l, ftile, N, nseg, tail, tname):
        s = 0
        L = N // nseg
        cs = cst[:, s, 0:nseg * (L + 3)].rearrange("p (g t) -> p g t", g=nseg)
        cn = "cst%d" % s
        cp("pool", cs[:, :, 0:3], tail.rearrange("p r g -> p g r"), [tname], [cn])
        cp("act", cs[:, :, 3:3 + L], ps.rearrange("p (g t) -> p g t", g=nseg), [bn], [cn])
        ft_, fn_ = next_ftmp()
        acc = ft_[:, 0:N].rearrange("p (g t) -> p g t", g=nseg)
        wbase = 56
        eng = "dve"
        ts(eng, acc, cs[:, :, 0:L], vecT[:, l, wbase + ftile:wbase + ftile + 1], vecT[:, l, 88 + ftile:88 + ftile + 1],
           ALU.mult, ALU.add, [cn, "vecT"], [fn_])
        for j in range(1, 4):
            stt("dve", acc, cs[:, :, j:j + L], vecT[:, l, wbase + j * 8 + ftile:wbase + j * 8 + ftile + 1], acc,
                ALU.mult, ALU.add, [cn, "vecT", fn_], [fn_])
        cp("pool", tail.rearrange("p r g -> p g r"), cs[:, :, L:L + 3], [cn], [tname])
        if ftile < 4:
            act(qmT[:, ftile, 0:N], ft_[:, 0:N], AF.Silu, [fn_], ["qmT"])
        else:
            act(ksilu[:, ftile - 4, 0:N], ft_[:, 0:N], AF.Silu, [fn_], ["ksilu"])

    def layer_setup(l):
        stt("dve", gsc1[:, :, :], modT[:, l, 8:16, :], 1.0, vecT[:, l, 48:56].unsqueeze(2).to_broadcast([128, 8, 5]),
            ALU.add, ALU.mult, ["modT", "vecT"], ["gsc"])
        stt("dve", gsc2[:, :, :], modT[:, l, 32:40, :], 1.0, vecT[:, l, 100:108].unsqueeze(2).to_broadcast([128, 8, 5]),
            ALU.add, ALU.mult, ["modT", "vecT"], ["gsc"])
        dma(bif[:, :], b_if.ap()[l].partition_broadcast(128), [], ["bif"], "bif")
        dma(Esb[:, 64:384], rel_bias.ap()[l], [], ["xin"], "xin")
        cp("dve", Esb[:, 0:64], Esb[:, 64:65].to_broadcast([8, 64]), ["xin"], ["xin"])
        cp("dve", Esb[:, 384:512], Esb[:, 383:384].to_broadcast([8, 128]), ["xin"], ["xin"])
        dma(Escr.ap()[l], Esb[:, :], ["xin"], ["Escr"], "Escr")
        ts("dve", dg[:, :], idf[0:8, 0:8], Esb[:, 511:512], None, ALU.mult, None, ["idf", "xin"], ["dg"])
        mm(psf[:, 0, 0:8], ones8[:, :], dg[:, :], True, True, ["ones8", "dg"], ["f0"])
        cp("dve", chead[:, :], psf[:, 0, 0:8], ["f0"], ["chead"])
        for kb in (2, 3, 4):
            src = bass.AP(Escr, l * NH * 512 + (4 - kb) * 128, [[1, 128], [512, NH], [1, 128]])
            dma(stmp[:, :, :], src, ["Escr"], ["xin"], "hank")
            sflat = stmp[:, :, :].rearrange("p h q -> p (h q)")
            for half in range(2):
                mm(psf[:, 4 + half, :], Jf[:, :], sflat[:, half * 512:(half + 1) * 512], True, True,
                   ["Jf", "xin"], ["f%d" % (4 + half)])
            cp("dve", btab[:, kb - 2, :, :], S2, ["f4", "f5"], ["btab"])
        memset("pool", state_p[:, :, :], 0.0, ["state_p"])
        memset("pool", m_p[:, :], 0.0, ["m_p"])
        memset("pool", ctail[:, :, :], 0.0, ["ctail"])

    def bias_table(kb, nk, nq):
        if kb < 2:
            return chead[0:nk, :].rearrange("p (s a) -> p a s", a=2).unsqueeze(3).to_broadcast([nk, 2, 4, nq]), ["chead"]
        return hview(btab[0:nk, kb - 2, :, 0:nq]), ["btab"]

    def proj_phase(l, N, groups, is_prompt, tile, last):
        for pi in range(8):
            c0, w = IN_PANELS[pi]
            panel, pname = load_panel("in", l, pi, 8, w)
            if pi in (0, 1, 3, 4, 6):
                for ct in range(4):
                    ps, bn = fm_proj(panel, pname, KT, ct, hT, "hT", N)
                    if pi == 0:
                        act(qaT[:, ct, 0:N], ps, AF.Identity, [bn], ["qaT"], scale=float(DH ** -0.5))
                    elif pi == 1:
                        if is_prompt:
                            pos0 = (tile % 2) * NT
                            cp("dve", kaT[:, ct, pos0:pos0 + N], ps, [bn], ["kaT"])
                        else:
                            cp("dve", kaT[:, ct, 0:N], ps, [bn], ["kaT"])
                    elif pi == 3:
                        if is_prompt:
                            conv_evac(ps, bn, l, ct, N, 1, ctail[:, :, ct:ct + 1], "ctail")
                        else:
                            conv_evac(ps, bn, l, ct, N, NSS, ctail_s[:, :, ct, :], "ctail_s")
                    elif pi == 4:
                        if is_prompt:
                            conv_evac(ps, bn, l, 4 + ct, N, 1, ctail[:, :, 4 + ct:5 + ct], "ctail")
                        else:
                            conv_evac(ps, bn, l, 4 + ct, N, NSS, ctail_s[:, :, 4 + ct, :], "ctail_s")
                    else:
                        ft_, fn_ = next_ftmp()
                        act(ft_[:, 0:N], ps, AF.Sigmoid, [bn], [fn_])
                        ts("dve", omg[:, ct, 0:N], ft_[:, 0:N], vecT[:, l, 96 + ct:97 + ct], None, ALU.mult, None,
                           [fn_, "vecT"], ["omg"])
                if pi == 1 and (last or not is_prompt):
                    for gi, (g0, gn) in enumerate(groups):
                        ps, bn = tm_proj(panel, pname, g0, gn, 512)
                        cp("dve", xin[0:gn, 0:512], ps, [bn], ["xin"])
                        dst = nkp.ap()[l, g0:g0 + gn, :] if is_prompt else nks.ap()[l, gi]
                        dma(dst, xin[0:gn, 0:512], ["xin"], [], "st_k")
            else:
                for gi, (g0, gn) in enumerate(groups):
                    ps, bn = tm_proj(panel, pname, g0, gn, w)
                    if pi == 2:
                        if is_prompt:
                            rb_ = (tile % 2) * 4 + gi
                        else:
                            rb_ = gi
                        cp("dve", Vring[0:gn, rb_, :, 0:DH], ps.rearrange("p (h d) -> p h d", d=DH), [bn], ["Vring"])
                        if last or not is_prompt:
                            cp("dve", xin[0:gn, 512:1024], ps, [bn], ["xin"])
                            dst = nvp.ap()[l, g0:g0 + gn, :] if is_prompt else nvs.ap()[l, gi]
                            dma(dst, xin[0:gn, 512:1024], ["xin"], [], "st_v")
                    elif pi == 5:
                        cp("act", vaug[0:gn, gi, :, 0:MD], ps.rearrange("p (h d) -> p h d", d=MD), [bn], ["vaug"])
                    else:
                        cp("dve", gate_tm[0:gn, gi, :], ps, [bn], ["gate_tm"])

    def merge_ffn_phase(l, N, xbuf, xname, segs):
        for half in range(2):
            pa, pan = load_panel("bra", l, half, 4, 512)
            pg, pgn = load_panel("in", l, 8 + half, 8, 512)
            for c in range(4):
                ct = half * 4 + c
                ba, ban = fm_proj(pa, pan, 4, c, yaT, "yaT", N)
                ga, gan = fm_proj(pg, pgn, KT, c, hT, "hT", N)
                f1, f1n = next_ftmp()
                act(f1[:, 0:N], ga, AF.Sigmoid, [gan], [f1n])
                tt("dve", f1[:, 0:N], ba, f1[:, 0:N], ALU.mult, [ban, f1n], [f1n])
                msv, msn = mstash(ct)
                cp("act", msv[:, 0:N], f1[:, 0:N], [f1n], [msn])
        P.mark("mf_a")
        for half in range(2):
            pm, pmn = load_panel("brm", l, half, 4, 512)
            pg, pgn = load_panel("in", l, 10 + half, 8, 512)
            for c in range(4):
                ct = half * 4 + c
                bm, bmn = fm_proj(pm, pmn, 4, c, ymT, "ymT", N)
                gm, gmn = fm_proj(pg, pgn, KT, c, hT, "hT", N)
                f1, f1n = next_ftmp()
                act(f1[:, 0:N], gm, AF.Sigmoid, [gmn], [f1n])
                tt("dve", f1[:, 0:N], bm, f1[:, 0:N], ALU.mult, [bmn, f1n], [f1n])
                msv, msn = mstash(ct)
                mgv, mgn = mergedv(ct)
                tt("dve", mgv[:, 0:N], f1[:, 0:N], msv[:, 0:N], ALU.add, [f1n, msn], [mgn])
        P.mark("mf_b")
        for half in range(2):
            po, pon = load_panel("out", l, half, 8, 512)
            for c in range(4):
                ct = half * 4 + c
                o, on = fm_proj(po, pon, KT, c, mergedv, None, N)
                for (c0, n, r) in segs:
                    stt("dve", xbuf[:, ct, c0:c0 + n], o[:, c0:c0 + n], modT[:, l, 16 + ct, r:r + 1],
                        xbuf[:, ct, c0:c0 + n], ALU.mult, ALU.add, [on, "modT", xname], [xname])
        P.mark("mf_out")
        norm_mod(xbuf, xname, N, segs, gsc2, 24, l)
        P.mark("mf_norm")
        for f in range(FT):
            pgu, pgun = load_panel("gu", l, f, 8, 256)
            g_, gn_ = fm_proj(pgu, pgun, KT, 0, hT, "hT", N)
            u_, un_ = fm_proj(pgu, pgun, KT, 1, hT, "hT", N)
            f1, f1n = next_ftmp()
            act(f1[:, 0:N], g_, AF.Silu, [gn_], [f1n])
            tt("dve", actT[:, f, 0:N], u_, f1[:, 0:N], ALU.mult, [un_, f1n], ["actT"])
        P.mark("mf_gu")
        for ct in range(8):
            pd, pdn = load_panel("dn", l, ct, FT, 128)
            o, on = fm_proj(pd, pdn, FT, 0, actT, "actT", N)
            for (c0, n, r) in segs:
                stt("dve", xbuf[:, ct, c0:c0 + n], o[:, c0:c0 + n], modT[:, l, 40 + ct, r:r + 1],
                    xbuf[:, ct, c0:c0 + n], ALU.mult, ALU.add, [on, "modT", xname], [xname])


    def final_out(xbuf, xname, N, ydst_fn):
        rms_stats([(xbuf[:, kt, 0:N], [xname]) for kt in range(KT)], N, 1.0 / D)
        for blk in range(N // 128):
            c0 = blk * 128
            for kt in range(KT):
                ft_, fn_ = next_ftmp()
                stt("dve", ft_[:, 0:128], xbuf[:, kt, c0:c0 + 128], vecT[:, 1, 108 + kt:109 + kt], rstd[:, c0:c0 + 128],
                    ALU.mult, ALU.mult, [xname, "vecT", "rstd"], [fn_])
                b = kt // 4
                tr(psf[:, b, (kt % 4) * 128:(kt % 4) * 128 + 128], ft_[:, 0:128], idf[:, :], [fn_, "idf"], ["f%d" % b])
            cp("act", xin[:, 0:512], psf[:, 0, :], ["f0"], ["xin"])
            cp("dve", xin[:, 512:1024], psf[:, 1, :], ["f1"], ["xin"])
            dma(ydst_fn(blk), xin[:, :], ["xin"], [], "st_y")

    def prompt_tile(l, j):
        last = (j == NTILES - 1)
        t0 = j * NT
        if l == 0:
            for blk in range(4):
                dma(xin[:, :], xp.ap()[t0 + blk * 128:t0 + (blk + 1) * 128, :], [], ["xin"], "ld_x")
                for kt in range(KT):
                    b = kt // 4
                    tr(psf[:, b, (kt % 4) * 128:(kt % 4) * 128 + 128], xin[:, kt * 128:(kt + 1) * 128], idf[:, :],
                       ["xin", "idf"], ["f%d" % b])
                cp("act", xT[:, 0:4, blk * 128:(blk + 1) * 128], psf[:, 0, :].rearrange("p (k t) -> p k t", k=4),
                   ["f0"], ["xT"])
                cp("dve", xT[:, 4:8, blk * 128:(blk + 1) * 128], psf[:, 1, :].rearrange("p (k t) -> p k t", k=4),
                   ["f1"], ["xT"])
        else:
            dma(xT[:, :, :], xscr.ap()[:, :, t0:t0 + NT].rearrange("k p t -> p k t"), ["xscr%d" % j], ["xT"], "ld_x")
        segs = [(0, NT, 0)]
        P.mark("p%d_%d_load" % (l, j))
        norm_mod(xT, "xT", NT, segs, gsc1, 0, l)
        P.mark("p%d_%d_norm" % (l, j))
        groups = [(b * 128, 128) for b in range(4)]
        proj_phase(l, NT, groups, True, j, last)
        P.mark("p%d_%d_proj" % (l, j))
        for pq in range(4):
            gb = j * 4 + pq
            pv = []
            for kb in range(5):
                gkb = gb - 4 + kb
                if gkb < 0:
                    continue
                rbk = gkb % 8
                table, trd = bias_table(kb, 128, 128)

                def lhs_fn(h, rbk=rbk):
                    return kaT[(h % 2) * 64:(h % 2) * 64 + 64, h // 2, rbk * 128:(rbk + 1) * 128], ["kaT"]
                attn_scores(kb, lhs_fn, 128, pq * 128, 128, table, trd)
                if kb == 0:
                    memset("pool", PT[0:64, 0, :, 64:128], 0.0, ["PT"])
                if kb == 4:
                    memset("pool", PT[64:128, 4, :, 0:64], 0.0, ["PT"])

                def rhs_fn(h, rbk=rbk):
                    return Vring[:, rbk, h, :], ["Vring"]
                pv.append((kb, 128, rhs_fn))
            attn_out(128, pv, pq * 128)
            P.mark("p%d_%d_attn%d" % (l, j, pq))
            mlstm_chunk(128, pq * 128, pq, state_p, "state_p", m_p, "m_p")
            P.mark("p%d_%d_mlstm%d" % (l, j, pq))
        merge_ffn_phase(l, NT, xT, "xT", segs)
        P.mark("p%d_%d_ffn" % (l, j))
        if l == 0:
            dma(xscr.ap()[:, :, t0:t0 + NT].rearrange("k p t -> p k t"), xT[:, :, :], ["xT"], ["xscr%d" % j], "st_x")
        else:
            final_out(xT, "xT", NT, lambda blk: yp.ap()[t0 + blk * 128:t0 + (blk + 1) * 128, :])
        if last:
            tr(psf[0:24, 2, 0:128], ctail[:, :, :].rearrange("p r f -> p (r f)"), idf[:, :], ["ctail", "idf"], ["f2"])
            cp("dve", xin[0:24, 0:128], psf[0:24, 2, 0:128], ["f2"], ["xin"])
            dma(ncp.ap()[l], xin[0:24, 0:128], ["xin"], [], "st_c")
            dma(nCp.ap()[l].rearrange("h k v -> k h v"), state_p[:, :, 0:MD], ["state_p"], [], "st_C")
            tr(psf[0:4, 2, 128:256], state_p[:, :, MD], idf[:, :], ["state_p", "idf"], ["f2"])
            cp("dve", xin[0:4, 128:256], psf[0:4, 2, 128:256], ["f2"], ["xin"])
            dma(nnp.ap()[l], xin[0:4, 128:256], ["xin"], [], "st_n")
            dma(nmp.ap()[l:l + 1, :], m_p[0:1, :], ["m_p"], [], "st_m")

    def sample_tile(l):
        N = NSS * TS
        if l == 0:
            dma(xin[:, :], xs.ap(), [], ["xin"], "ld_x")
            for kt in range(KT):
                b = kt // 4
                tr(psf[:, b, (kt % 4) * 128:(kt % 4) * 128 + 128], xin[:, kt * 128:(kt + 1) * 128], idf[:, :],
                   ["xin", "idf"], ["f%d" % b])
            cp("act", xsT[:, 0:4, :], psf[:, 0, :].rearrange("p (k t) -> p k t", k=4), ["f0"], ["xsT"])
            cp("dve", xsT[:, 4:8, :], psf[:, 1, :].rearrange("p (k t) -> p k t", k=4), ["f1"], ["xsT"])
        segs = [(s * TS, TS, 1 + s) for s in range(NSS)]
        for s in range(NSS):
            dma(xin[0:24, 0:128], sconv.ap()[l, s], [], ["xin"], "ld_sc")
            tr(psf[:, 2, 0:24], xin[0:24, 0:128], idf[0:24, 0:24], ["xin", "idf"], ["f2"])
            cp("dve", ctail_s[:, :, :, s], psf[:, 2, 0:24].rearrange("p (r f) -> p r f", r=3), ["f2"], ["ctail_s"])
        norm_mod(xsT, "xsT", N, segs, gsc1, 0, l)
        groups = [(s * TS, TS) for s in range(NSS)]
        proj_phase(l, N, groups, False, 0, False)
        for s in range(NSS):
            q0 = s * TS
            memset("pool", cvb[:, :, :, DH:DH + 1], 1.0, ["actT"])
            dma(ckb, ck.ap()[l, s].rearrange("(kb p) f -> p kb f", p=128), [], ["actT"], "ld_ck", eng="pool")
            for kb in range(4):
                dma(cvb[:, kb, :, 0:DH], cv.ap()[l, s][kb * 128:(kb + 1) * 128, :].rearrange("p (h d) -> p h d", d=DH),
                    [], ["actT"], "ld_cv", eng="pool")
            for kb in range(4):
                for f in range(4):
                    tr(psb[:, kb % 2, f * 128:(f + 1) * 128], ckb[:, kb, f * 128:(f + 1) * 128], idb[:, :],
                       ["actT", "idb"], ["tb%d" % (kb % 2)])
                cp("act" if kb % 2 == 0 else "dve", ckT[:, :, kb * 128:(kb + 1) * 128],
                   psb[:, kb % 2, :].rearrange("p (f k) -> p f k", f=4), ["tb%d" % (kb % 2)], ["actT"])
            pv = []
            for kb in range(4):
                table, trd = bias_table(kb, 128, TS)

                def lhs_fn(h, kb=kb):
                    return ckT[(h % 2) * 64:(h % 2) * 64 + 64, h // 2, kb * 128:(kb + 1) * 128], ["actT"]
                attn_scores(kb, lhs_fn, 128, q0, TS, table, trd)

                def rhs_fn(h, kb=kb):
                    return cvb[:, kb, h, :], ["actT"]
                pv.append((kb, 128, rhs_fn))
            table, trd = bias_table(4, TS, TS)

            def lhs_new(h, q0=q0):
                return kaT[(h % 2) * 64:(h % 2) * 64 + 64, h // 2, q0:q0 + TS], ["kaT"]
            attn_scores(4, lhs_new, TS, q0, TS, table, trd)

            def rhs_new(h, s=s):
                return Vring[0:TS, s, h, :], ["Vring"]
            pv.append((4, TS, rhs_new))
            attn_out(TS, pv, q0)
            dma(state_s[:, :, 0:MD], sC.ap()[l, s].rearrange("h k v -> k h v"), [], ["state_s"], "ld_sC")
            dma(xin[0:4, 128:256], sn.ap()[l, s], [], ["xin"], "ld_sn")
            tr(psf[:, 2, 32:36], xin[0:4, 128:256], idf[0:4, 0:4], ["xin", "idf"], ["f2"])
            cp("dve", state_s[:, :, MD], psf[:, 2, 32:36], ["f2"], ["state_s"])
            dma(m_s[:, :], sm.ap()[l, s * MH:(s + 1) * MH].partition_broadcast(128), [], ["m_s"], "ld_sm")
            mlstm_chunk(TS, q0, s, state_s, "state_s", m_s, "m_s")
            dma(nCs.ap()[l, s].rearrange("h k v -> k h v"), state_s[:, :, 0:MD], ["state_s"], [], "st_C")
            tr(psf[0:4, 2, 128:256], state_s[:, :, MD], idf[:, :], ["state_s", "idf"], ["f2"])
            cp("dve", xin[0:4, 256:384], psf[0:4, 2, 128:256], ["f2"], ["xin"])
            dma(nns.ap()[l, s], xin[0:4, 256:384], ["xin"], [], "st_n")
            dma(nms.ap()[l, s:s + 1, :], m_s[0:1, :], ["m_s"], [], "st_m")
        for s in range(NSS):
            cp("dve", stmp[:, 0, 0:24].rearrange("p (r f) -> p r f", r=3), ctail_s[:, :, :, s], ["ctail_s"], ["xin"])
            tr(psf[0:24, 2, 0:128], stmp[:, 0, 0:24], idf[:, :], ["xin", "idf"], ["f2"])
            cp("dve", xin[0:24, 384:512], psf[0:24, 2, 0:128], ["f2"], ["xin"])
            dma(ncs.ap()[l, s], xin[0:24, 384:512], ["xin"], [], "st_c")
        merge_ffn_phase(l, N, xsT, "xsT", segs)
        if l == 1:
            final_out(xsT, "xsT", N, lambda blk: ys.ap())

    P.mark("pre")
    for l in range(2):
        layer_setup(l)
        P.mark("setup%d" % l)
        for j in range(NTILES):
            prompt_tile(l, j)
            P.mark("ptile%d_%d" % (l, j))
        sample_tile(l)
        P.mark("stile%d" % l)
    import os
    lim = os.environ.get("KLIMIT")
    if lim:
        d = {}
        for k_, v_ in P.marks:
            d.setdefault(k_, v_)
        P.limit = d[lim] if lim in d else int(lim)
    print("marks", P.marks)
    P.build()
    return nc, P


_CACHE = {}


def kernel(x_prompt, x_sample, cache_k, cache_v, state_conv, state_C, state_n, state_m,
           c_prompt, c_sample, w_ada, b_ada, g_mix, w_in, b_if, conv_w, conv_b, rel_bias,
           mh_gain, w_br_att, w_br_mlstm, w_out, g_ffn, w_gate_up, w_down, g_final):
    f = lambda a: np.ascontiguousarray(np.asarray(a, dtype=np.float32))
    x_prompt = f(x_prompt); x_sample = f(x_sample)
    B, SEQ, _ = x_prompt.shape
    NCORE = 8
    if SEQ not in _CACHE:
        _CACHE[SEQ] = build_program(SEQ)[0]
    nc = _CACHE[SEQ]
    vecs = np.zeros((2, 116, 128), np.float32)
    for l in range(2):
        vecs[l, 0:48] = f(b_ada)[l].reshape(48, 128)
        vecs[l, 48:56] = f(g_mix)[l].reshape(8, 128)
        vecs[l, 56:88] = f(conv_w)[l].reshape(32, 128)
        vecs[l, 88:96] = f(conv_b)[l].reshape(8, 128)
        vecs[l, 96:100] = f(mh_gain)[l].reshape(4, 128)
        vecs[l, 100:108] = f(g_ffn)[l].reshape(8, 128)
        vecs[l, 108:116] = f(g_final).reshape(8, 128)
    shared = dict(vecs=vecs, b_if=f(b_if), rel_bias=f(rel_bias), w_ada=f(w_ada), w_in=f(w_in),
                  w_bra=f(w_br_att), w_brm=f(w_br_mlstm), w_out=f(w_out), w_gu=f(w_gate_up), w_dn=f(w_down))
    cache_k = f(cache_k); cache_v = f(cache_v); state_conv = f(state_conv)
    state_C = f(state_C); state_n = f(state_n); state_m = f(state_m)
    c_prompt = f(c_prompt); c_sample = f(c_sample)
    in_maps = []
    for c in range(NCORE):
        b = c % B
        s0, s1 = c * NSS, (c + 1) * NSS
        m = dict(shared)
        m["xp"] = x_prompt[b]
        m["xs"] = x_sample[s0:s1].reshape(NSS * TS, D)
        m["c5"] = np.ascontiguousarray(np.concatenate([c_prompt[b:b + 1], c_sample[s0:s1]], 0))
        m["ck"] = np.ascontiguousarray(cache_k[:, s0:s1].reshape(2, NSS, 512, 512))
        m["cv"] = np.ascontiguousarray(cache_v[:, s0:s1].reshape(2, NSS, 512, 512))
        m["sconv"] = np.ascontiguousarray(state_conv[:, s0:s1].reshape(2, NSS, 24, 128))
        m["sC"] = np.ascontiguousarray(state_C[:, s0:s1])
        m["sn"] = np.ascontiguousarray(state_n[:, s0:s1])
        m["sm"] = np.ascontiguousarray(state_m[:, s0:s1].reshape(2, NSS * MH))
        in_maps.append(m)
    res = run_bass_kernel_spmd(nc, in_maps, core_ids=list(range(NCORE)))
    R = res.results
    keep = 512
    y_prompt = np.stack([R[b]["yp"] for b in range(B)], 0)
    y_sample = np.concatenate([R[c]["ys"].reshape(NSS, TS, D) for c in range(NCORE)], 0)
    nkp = np.stack([R[b]["nkp"] for b in range(B)], 1).reshape(2, B, keep, NH, DH)
    nvp = np.stack([R[b]["nvp"] for b in range(B)], 1).reshape(2, B, keep, NH, DH)
    ncp = np.stack([R[b]["ncp"] for b in range(B)], 1).reshape(2, B, 3, 1024)
    nCp = np.stack([R[b]["nCp"] for b in range(B)], 1)
    nnp_ = np.stack([R[b]["nnp"] for b in range(B)], 1)
    nmp = np.stack([R[b]["nmp"] for b in range(B)], 1)
    nks = np.concatenate([R[c]["nks"] for c in range(NCORE)], 1).reshape(2, NCORE * NSS, TS, NH, DH)
    nvs = np.concatenate([R[c]["nvs"] for c in range(NCORE)], 1).reshape(2, NCORE * NSS, TS, NH, DH)
    ncs = np.concatenate([R[c]["ncs"] for c in range(NCORE)], 1).reshape(2, NCORE * NSS, 3, 1024)
    nCs = np.concatenate([R[c]["nCs"] for c in range(NCORE)], 1)
    nns = np.concatenate([R[c]["nns"] for c in range(NCORE)], 1)
    nms = np.concatenate([R[c]["nms"] for c in range(NCORE)], 1)
    outs = (y_prompt, y_sample, nkp, nvp, ncp, nCp, nnp_, nmp, nks, nvs, ncs, nCs, nns, nms)
    return tuple(np.ascontiguousarray(o, dtype=np.float32) for o in outs)
```

```python
import contextlib
import numpy as np
import concourse.bass as bass
import concourse.mybir as mybir
from concourse.bass_utils import run_bass_kernel_spmd

F32 = mybir.dt.float32
BF16 = mybir.dt.bfloat16
AF = mybir.ActivationFunctionType
ALU = mybir.AluOpType
AX = mybir.AxisListType

D = 1024
KT = 8
NH = 8
DH = 64
MH = 4
MD = 128
DFF = 2816
FT = 22
INW = 5640
NT = 512
NSS = 4
TS = 32
EPS = 1e-6
NEGM = -30000.0


class Prog:
    ENGS = ("pe", "act", "dve", "pool", "sp")

    def __init__(self, nc):
        self.nc = nc
        self.ops = []
        self.deferred = []
        self.marks = []
        self.limit = None

    def op(self, eng, fn, reads=(), writes=(), dma_key=None, defer=False, wait_total=()):
        o = dict(eng=eng, fn=fn, reads=tuple(reads), writes=tuple(writes), dma_key=dma_key,
                 wait_total=tuple(wait_total))
        if defer:
            self.deferred.append(o)
        else:
            self.ops.append(o)

    def mark(self, name):
        self.flush()
        self.marks.append((name, len(self.ops)))

    def flush(self):
        self.ops.extend(self.deferred)
        self.deferred = []

    def build(self):
        self.flush()
        nc = self.nc
        if self.limit is not None:
            self.ops = self.ops[:self.limit]
        ops = self.ops

        def stream(o):
            return ("dma", o["dma_key"]) if o["dma_key"] is not None else o["eng"]

        last_w = {}
        readers = {}
        pos = []
        cnt = {}
        seen = {e: {} for e in self.ENGS}
        need = []
        by_sp = {}
        for i, o in enumerate(ops):
            s = stream(o)
            cnt[s] = cnt.get(s, 0) + 1
            pos.append(cnt[s])
            by_sp[(s, cnt[s])] = i
            deps = set()
            for r in o["reads"]:
                if r in last_w:
                    deps.add(last_w[r])
            for r in o["writes"]:
                if r in last_w:
                    deps.add(last_w[r])
                deps.update(readers.get(r, ()))
            deps.discard(i)
            for r in o["reads"]:
                readers.setdefault(r, []).append(i)
            for r in o["writes"]:
                last_w[r] = i
                readers[r] = []
            w = {}
            for d in deps:
                sd = stream(ops[d])
                if sd == "pe" and o["eng"] == "pe" and o["dma_key"] is None:
                    continue
                if pos[d] > w.get(sd, 0):
                    w[sd] = pos[d]
            lst = []
            sn = seen[o["eng"]]
            for sd, p in w.items():
                if sn.get(sd, 0) >= p:
                    continue
                sn[sd] = p
                lst.append((sd, p))
            need.append(lst)
        marked = set()
        for lst in need:
            for sd, p in lst:
                if not isinstance(sd, tuple):
                    marked.add(by_sp[(sd, p)])
        val = {}
        run = {}
        wtot = {}
        for i, o in enumerate(ops):
            s = stream(o)
            lst = []
            for k in o["wait_total"]:
                sk = ("dma", k)
                v = run.get(sk, 0)
                if v > seen[o["eng"]].get(("tot", k), 0):
                    seen[o["eng"]][("tot", k)] = v
                    lst.append((sk, v))
            wtot[i] = lst
            if o["dma_key"] is not None:
                run[s] = run.get(s, 0) + 16
                val[i] = run[s]
            elif i in marked:
                run[s] = run.get(s, 0) + 1
                val[i] = run[s]
        streams = sorted(set(stream(o) for o in ops), key=str)
        self.n_ops = len(ops)
        with contextlib.ExitStack() as es:
            sems = {}
            for k, s in enumerate(streams):
                sems[s] = es.enter_context(nc.semaphore("sem%d" % k))
            block = es.enter_context(nc.Block())
            hooks = {"pe": block.tensor, "act": block.scalar, "dve": block.vector,
                     "pool": block.gpsimd, "sp": block.sync}
            for e in self.ENGS:
                idxs = [i for i, o in enumerate(ops) if o["eng"] == e]

                def body(eng, idxs=idxs, e=e):
                    for i in idxs:
                        o = ops[i]
                        for sd, p in need[i]:
                            eng.wait_ge(sems[sd], val[by_sp[(sd, p)]])
                        for sk, v in wtot[i]:
                            eng.wait_ge(sems[sk], v)
                        ins = o["fn"](eng)
                        if o["dma_key"] is not None:
                            ins.then_inc(sems[stream(o)], 16)
                        elif i in marked:
                            ins.then_inc(sems[stream(o)], 1)
                    if e == "sp":
                        for s, v in run.items():
                            if isinstance(s, tuple):
                                eng.wait_ge(sems[s], v)
                hooks[e](body)


def build_program(SEQ):
    assert SEQ % NT == 0
    NTILES = SEQ // NT
    nc = bass.Bass("TRN2", target_bir_lowering=False)
    P = Prog(nc)

    def din(name, shape):
        return nc.dram_tensor(name, list(shape), F32, kind="ExternalInput")

    def dout(name, shape):
        return nc.dram_tensor(name, list(shape), F32, kind="ExternalOutput")

    xp = din("xp", [SEQ, D])
    xs = din("xs", [NSS * TS, D])
    c5 = din("c5", [1 + NSS, D])
    ck = din("ck", [2, NSS, 512, 512])
    cv = din("cv", [2, NSS, 512, 512])
    sconv = din("sconv", [2, NSS, 24, 128])
    sC = din("sC", [2, NSS, MH, MD, MD])
    sn = din("sn", [2, NSS, MH, MD])
    sm = din("sm", [2, NSS * MH])
    vecs = din("vecs", [2, 116, 128])
    b_if = din("b_if", [2, 8])
    rel_bias = din("rel_bias", [2, NH, 320])
    w_ada = din("w_ada", [2, D, 6 * D])
    w_in = din("w_in", [2, D, INW])
    w_bra = din("w_bra", [2, 512, D])
    w_brm = din("w_brm", [2, 512, D])
    w_out = din("w_out", [2, D, D])
    w_gu = din("w_gu", [2, D, 2 * DFF])
    w_dn = din("w_dn", [2, DFF, D])

    yp = dout("yp", [SEQ, D])
    ys = dout("ys", [NSS * TS, D])
    nkp = dout("nkp", [2, 512, 512])
    nvp = dout("nvp", [2, 512, 512])
    ncp = dout("ncp", [2, 24, 128])
    nCp = dout("nCp", [2, MH, MD, MD])
    nnp = dout("nnp", [2, MH, MD])
    nmp = dout("nmp", [2, MH])
    nks = dout("nks", [2, NSS, TS, 512])
    nvs = dout("nvs", [2, NSS, TS, 512])
    ncs = dout("ncs", [2, NSS, 24, 128])
    nCs = dout("nCs", [2, NSS, MH, MD, MD])
    nns = dout("nns", [2, NSS, MH, MD])
    nms = dout("nms", [2, NSS, MH])

    SL = 4096
    wsc = {
        "in": nc.dram_tensor("wsc_in", [2, 12, 128, SL], BF16),
        "bra": nc.dram_tensor("wsc_bra", [2, 2, 128, SL], BF16),
        "brm": nc.dram_tensor("wsc_brm", [2, 2, 128, SL], BF16),
        "out": nc.dram_tensor("wsc_out", [2, 2, 128, SL], BF16),
        "gu": nc.dram_tensor("wsc_gu", [2, FT, 128, SL], BF16),
        "dn": nc.dram_tensor("wsc_dn", [2, 8, 128, SL], BF16),
    }
    xscr = nc.dram_tensor("xscr", [KT, 128, SEQ], F32)
    Escr = nc.dram_tensor("Escr", [2, NH, 512], F32)

    def sb(name, shape, dt=F32):
        return nc.alloc_sbuf_tensor(name, list(shape), dt)

    idb = sb("idb", [128, 128], BF16)
    idf = sb("idf", [128, 128])
    Jf = sb("Jf", [128, 128])
    ones_bf = sb("ones_bf", [128, 128], BF16)
    tri = sb("tri", [128, 128], BF16)
    vecT = sb("vecT", [128, 2, 116])
    modT = sb("modT", [128, 2, 48, 5])
    gsc1 = sb("gsc1", [128, 8, 5])
    gsc2 = sb("gsc2", [128, 8, 5])
    bif = sb("bif", [128, 8])
    scT = sb("scT", [128, 8, 5])
    dg = sb("dg", [8, 8])
    ones8 = sb("ones8", [8, 128])
    chead = sb("chead", [128, 8])

    xT = sb("xT", [128, KT, NT])
    xsT = sb("xsT", [128, KT, 128])
    xin = sb("xin", [128, D])
    hT = sb("hT", [128, KT, NT], BF16)
    sq = sb("sq", [128, 2, NT], BF16)
    rstd = sb("rstd", [128, NT])
    ftmp = sb("ftmp", [128, 3, NT])
    wring = sb("wring", [128, 3, SL], BF16)
    qaT = sb("qaT", [128, 4, NT], BF16)
    kaT = sb("kaT", [128, 4, 2 * NT], BF16)
    Vring = sb("Vring", [128, 8, NH, DH + 1], BF16)
    cst = sb("cst", [128, 1, NT + 4])
    ctail = sb("ctail", [128, 3, 8])
    ctail_s = sb("ctail_s", [128, 3, 8, NSS])
    qmT = sb("qmT", [128, 4, NT], BF16)
    ksilu = sb("ksilu", [128, 4, NT], BF16)
    omg = sb("omg", [128, 4, NT], BF16)
    vaug = sb("vaug", [128, 4, MH, MD + 1], BF16)
    gate_tm = sb("gate_tm", [128, 4, 8])
    btab = sb("btab", [128, 3, NH, 128])
    PT = sb("PT", [128, 5, NH, 128], BF16)
    rden = sb("rden", [128, NH])
    ya = sb("ya", [128, NH, DH], BF16)
    yaT = sb("yaT", [128, 4, NT], BF16)
    gli = sb("gli", [128, 4])
    gsp = sb("gsp", [128, 4])
    ghl = sb("ghl", [128, 4, 4], BF16)
    gres = sb("gres", [128, 4])
    reps = sb("reps", [128, 4, MH, 128], BF16)
    Rm = sb("Rm", [128, 4])
    cm = sb("cm", [128, 4])
    mtmp = sb("mtmp", [128, 4])
    a_bc = sb("a_bc", [128, MH, 128])
    clampv = sb("clampv", [128, MH, 128])
    kpT = sb("kpT", [128, MH, 128], BF16)
    ktm = sb("ktm", [128, MH, 128], BF16)
    Sc_bf = sb("Sc_bf", [128, MH, MD + 1], BF16)
    nrep = sb("nrep", [128, MH, 128], BF16)
    STm = sb("STm", [128, MH, 128], BF16)
    t1 = sb("t1", [128, MH, 128])
    t2 = sb("t2", [128, MH, 128])
    sqh = sb("sqh", [128, MH, 128], BF16)
    state_p = sb("state_p", [128, MH, MD + 1])
    m_p = sb("m_p", [128, 4])
    state_s = sb("state_s", [128, MH, MD + 1])
    m_s = sb("m_s", [128, 4])
    ymT = sb("ymT", [128, 4, NT], BF16)
    actT = sb("actT", [128, FT, NT], BF16)
    ckb = actT[:, 0:4, :]
    ckT = actT[:, 4:8, :]
    cvb = actT[:, 8:13, :].rearrange("p a b -> p (a b)")[:, 0:4 * NH * 65].rearrange(
        "p (k h d) -> p k h d", k=4, h=NH)

    c5sb = xin[0:8, :]
    vrow = xin[:, 0:128]
    Esb = xin[0:8, 0:512]
    stmp = xin[:, :].rearrange("p (h q) -> p h q", h=NH)
    hm = a_bc
    ones_f = ftmp[:, 0, 0:128]
    tri_f = ftmp[:, 1, 0:128]

    def mstash(ct):
        return (qaT[:, ct, :], "qaT") if ct < 4 else (qmT[:, ct - 4, :], "qmT")

    def mergedv(ct):
        return (ksilu[:, ct, :], "ksilu") if ct < 4 else (omg[:, ct - 4, :], "omg")

    psf = nc.alloc_psum_tensor("psf", [128, 7, 512], F32)
    psb = nc.alloc_psum_tensor("psb", [128, 2, 512], BF16)

    def fb(b):
        return psf[:, b, :]

    def mm(out, lhsT, rhs, start, stop, rd, wr):
        P.op("pe", lambda e: e.matmul(out, lhsT=lhsT, rhs=rhs, start=start, stop=stop), rd, wr)

    def tr(out, in_, ident, rd, wr):
        P.op("pe", lambda e: e.transpose(out=out, in_=in_, identity=ident), rd, wr)

    def act(out, in_, func, rd, wr, bias=0.0, scale=1.0):
        P.op("act", lambda e: e.activation(out=out, in_=in_, func=func, bias=bias, scale=scale), rd, wr)

    def cp(eng, out, in_, rd, wr):
        if eng == "act":
            P.op("act", lambda e: e.activation(out=out, in_=in_, func=AF.Identity), rd, wr)
        else:
            P.op(eng, lambda e: e.tensor_copy(out=out, in_=in_), rd, wr)

    def tt(eng, out, in0, in1, op, rd, wr):
        P.op(eng, lambda e: e.tensor_tensor(out=out, in0=in0, in1=in1, op=op), rd, wr)

    def ts(eng, out, in0, s1, s2, op0, op1, rd, wr):
        if s2 is None:
            P.op(eng, lambda e: e.tensor_scalar(out=out, in0=in0, scalar1=s1, scalar2=None, op0=op0), rd, wr)
        else:
            P.op(eng, lambda e: e.tensor_scalar(out=out, in0=in0, scalar1=s1, scalar2=s2, op0=op0, op1=op1), rd, wr)

    def stt(eng, out, in0, scalar, in1, op0, op1, rd, wr):
        P.op(eng, lambda e: e.scalar_tensor_tensor(out=out, in0=in0, scalar=scalar, in1=in1, op0=op0, op1=op1), rd, wr)

    def recip(out, in_, rd, wr):
        P.op("dve", lambda e: e.reciprocal(out=out, in_=in_), rd, wr)

    def memset(eng, ap, v, wr):
        P.op(eng, lambda e: e.memset(ap, v), (), wr)

    def dma(out, in_, rd, wr, key, eng="sp", defer=False, wait_total=()):
        P.op(eng, lambda e: e.dma_start(out=out, in_=in_), rd, wr, dma_key=key, defer=defer, wait_total=wait_total)

    rr = {"ft": 0, "acc": 0, "w": 0}

    def next_ftmp():
        i = rr["ft"] % 3
        rr["ft"] += 1
        return ftmp[:, i, :], "ftmp%d" % i

    def next_acc():
        i = rr["acc"] % 4
        rr["acc"] += 1
        return i, "f%d" % i

    memset("pool", idf[:], 0.0, ["idf"])
    P.op("pool", lambda e: e.affine_select(out=idf[:], in_=idf[:], pattern=[[-1, 128]], compare_op=ALU.not_equal,
                                           fill=1.0, base=0, channel_multiplier=1), ["idf"], ["idf"])
    memset("pool", Jf[:], 0.0, ["Jf"])
    P.op("pool", lambda e: e.affine_select(out=Jf[:], in_=Jf[:], pattern=[[1, 128]], compare_op=ALU.not_equal,
                                           fill=1.0, base=-127, channel_multiplier=1), ["Jf"], ["Jf"])
    memset("pool", tri_f, 1.0, ["ftmp1"])
    P.op("pool", lambda e: e.affine_select(out=tri_f, in_=tri_f, pattern=[[1, 128]], compare_op=ALU.is_ge,
                                           fill=0.0, base=0, channel_multiplier=-1), ["ftmp1"], ["ftmp1"])
    memset("pool", ones_f, 1.0, ["ftmp0"])
    memset("pool", ones8[:, :], 1.0, ["ones8"])
    cp("dve", idb[:], idf[:], ["idf"], ["idb"])
    cp("dve", tri[:], tri_f, ["ftmp1"], ["tri"])
    cp("dve", ones_bf[:], ones_f, ["ftmp0"], ["ones_bf"])
    memset("pool", Vring[:, :, :, DH:DH + 1], 1.0, ["Vring"])
    memset("pool", vaug[:, :, :, MD:MD + 1], 1.0, ["vaug"])
    memset("pool", cvb[:, :, :, DH:DH + 1], 1.0, ["actT"])

    def conv_panel(dst, src_cols, kt, pw, key):
        src = src_cols.rearrange("(kt p) c -> p kt c", p=128)
        d = dst[:, 0:kt * pw].rearrange("p (k c) -> p k c", k=kt)
        dma(d, src, [], [], key, eng="pool")

    IN_PANELS = [(512 * i, 512) for i in range(7)] + [(3584, 8), (3592, 512), (4104, 512), (4616, 512), (5128, 512)]

    def convert_layer(l):
        for pi, (c0, w) in enumerate(IN_PANELS):
            conv_panel(wsc["in"].ap()[l, pi], w_in.ap()[l][:, c0:c0 + w], 8, w, "cv%d_in" % l)
        for pi in range(2):
            conv_panel(wsc["bra"].ap()[l, pi], w_bra.ap()[l][:, pi * 512:(pi + 1) * 512], 4, 512, "cv%d_bra" % l)
            conv_panel(wsc["brm"].ap()[l, pi], w_brm.ap()[l][:, pi * 512:(pi + 1) * 512], 4, 512, "cv%d_brm" % l)
            conv_panel(wsc["out"].ap()[l, pi], w_out.ap()[l][:, pi * 512:(pi + 1) * 512], 8, 512, "cv%d_out" % l)
        for f in range(FT):
            key = "cv%d_gu" % l
            dstp = wsc["gu"].ap()[l, f][:, 0:8 * 256].rearrange("p (k c) -> p k c", k=8)
            srcg = w_gu.ap()[l][:, f * 128:(f + 1) * 128].rearrange("(kt p) c -> p kt c", p=128)
            srcu = w_gu.ap()[l][:, DFF + f * 128:DFF + (f + 1) * 128].rearrange("(kt p) c -> p kt c", p=128)
            dma(dstp[:, :, 0:128], srcg, [], [], key, eng="pool")
            dma(dstp[:, :, 128:256], srcu, [], [], key, eng="pool")
        for c in range(8):
            conv_panel(wsc["dn"].ap()[l, c], w_dn.ap()[l][:, c * 128:(c + 1) * 128], FT, 128, "cv%d_dn" % l)

    convert_layer(0)
    convert_layer(1)

    GRAN = 1024
    NGR = (3 * SL) // GRAN
    wflat = wring[:, :, :].rearrange("p a b -> p (a b)")

    def load_panel(which, l, pi, kt, pw):
        n = kt * pw
        g = -(-n // GRAN)
        start = rr["w"]
        if start + g > NGR:
            start = 0
        rr["w"] = (start + g) % NGR
        names = ["wg%d" % i for i in range(start, start + g)]
        dst = wflat[:, start * GRAN:start * GRAN + n]
        dma(dst, wsc[which].ap()[l, pi][:, 0:n], [], names, "wl%d" % start,
            wait_total=["cv%d_%s" % (l, which)])
        return dst.rearrange("p (k c) -> p k c", k=kt), names

    for l in range(2):
        dma(vrow[0:116, :], vecs.ap()[l], [], ["xin"], "xin")
        tr(fb(0)[:, 0:116], vrow[0:116, :], idf[0:116, 0:116], ["xin", "idf"], ["f0"])
        cp("dve", vecT[:, l, :], fb(0)[:, 0:116], ["f0"], ["vecT"])
    dma(c5sb[0:5, :], c5.ap(), [], ["xin"], "c5")
    act(c5sb[0:5, :], c5sb[0:5, :], AF.Silu, ["xin"], ["xin"])
    for kt in range(KT):
        tr(fb(1)[:, kt * 8:kt * 8 + 5], c5sb[0:5, kt * 128:(kt + 1) * 128], idf[0:5, 0:5], ["xin", "idf"], ["f1"])
    cp("dve", scT[:], fb(1)[:, 0:64].rearrange("p (k r) -> p k r", r=8)[:, :, 0:5], ["f1"], ["scT"])
    for l in range(2):
        for pi in range(24):
            s = pi % 2
            wa = xT[:, :, s * 256:(s + 1) * 256]
            dma(wa, w_ada.ap()[l][:, pi * 256:(pi + 1) * 256].rearrange("(kt p) c -> p kt c", p=128),
                [], ["xTa%d" % s, "xT"], "wada%d" % s)
            for c in range(2):
                t = pi * 2 + c
                bi, bn = next_acc()
                for kt in range(KT):
                    mm(fb(bi)[:, 0:5], wa[:, kt, c * 128:(c + 1) * 128], scT[:, kt, :], kt == 0, kt == KT - 1,
                       ["xTa%d" % s, "scT"], [bn])
                ts("dve", modT[:, l, t, :], fb(bi)[:, 0:5], vecT[:, l, t:t + 1], None, ALU.add, None,
                   [bn, "vecT"], ["modT"])

    def rms_stats(xv, N, invd):
        nk = len(xv)
        for kt in range(nk):
            s = kt % 2
            act(sq[:, s, 0:N], xv[kt][0], AF.Square, xv[kt][1], ["sq%d" % s])
            mm(psf[:, 4, 0:N], ones_bf[:], sq[:, s, 0:N], kt == 0, kt == nk - 1, ["sq%d" % s, "ones_bf"], ["f4"])
        act(rstd[:, 0:N], psf[:, 4, 0:N], AF.Sqrt, ["f4"], ["rstd"], bias=EPS, scale=invd)
        recip(rstd[:, 0:N], rstd[:, 0:N], ["rstd"], ["rstd"])

    def norm_mod(xbuf, xname, N, segs, gsc, shbase, l):
        rms_stats([(xbuf[:, kt, 0:N], [xname]) for kt in range(KT)], N, 1.0 / D)
        for kt in range(KT):
            ft_, fn_ = next_ftmp()
            for (c0, n, r) in segs:
                stt("dve", ft_[:, c0:c0 + n], xbuf[:, kt, c0:c0 + n], gsc[:, kt, r:r + 1], rstd[:, c0:c0 + n],
                    ALU.mult, ALU.mult, [xname, "rstd", "gsc"], [fn_])
            for (c0, n, r) in segs:
                act(hT[:, kt, c0:c0 + n], ft_[:, c0:c0 + n], AF.Identity, [fn_, "modT"], ["hT"],
                    bias=modT[:, l, shbase + kt, r:r + 1], scale=1.0)

    def fm_proj(panel, pname, kt_n, ct, rhs_buf, rhs_name, N):
        bi, bn = next_acc()
        for kt in range(kt_n):
            if callable(rhs_buf):
                rv, rn = rhs_buf(kt)
                rv = rv[:, 0:N]
            else:
                rv, rn = rhs_buf[:, kt, 0:N], rhs_name
            mm(fb(bi)[:, 0:N], panel[:, kt, ct * 128:(ct + 1) * 128], rv, kt == 0, kt == kt_n - 1,
               list(pname) + [rn], [bn])
        return fb(bi)[:, 0:N], bn

    def tm_proj(panel, pname, g0, gn, ncols):
        bi, bn = next_acc()
        for kt in range(KT):
            mm(psf[0:gn, bi, 0:ncols], hT[:, kt, g0:g0 + gn], panel[:, kt, 0:ncols], kt == 0, kt == KT - 1,
               list(pname) + ["hT"], [bn])
        return psf[0:gn, bi, 0:ncols], bn

    def mlstm_chunk(T, c0, blk, state, sname, mst, mname):
        npart = T
        g = gate_tm[0:npart, blk, :]
        tt("dve", gli[0:npart, :], g[:, 0:4], bif[0:npart, 0:4], ALU.add, ["gate_tm", "bif"], ["gli"])
        tt("dve", gsp[0:npart, :], g[:, 4:8], bif[0:npart, 4:8], ALU.add, ["gate_tm", "bif"], ["gsp"])
        act(gsp[0:npart, :], gsp[0:npart, :], AF.Exp, ["gsp"], ["gsp"], scale=-1.0)
        act(gsp[0:npart, :], gsp[0:npart, :], AF.Ln, ["gsp"], ["gsp"], bias=1.0)
        cp("dve", ghl[0:npart, 0, :], gli[0:npart, :], ["gli"], ["ghl"])
        tt("dve", gres[0:npart, :], gli[0:npart, :], ghl[0:npart, 0, :], ALU.subtract, ["gli", "ghl"], ["gres"])
        cp("dve", ghl[0:npart, 1, :], gres[0:npart, :], ["gres"], ["ghl"])
        cp("dve", ghl[0:npart, 2, :], gsp[0:npart, :], ["gsp"], ["ghl"])
        tt("dve", gres[0:npart, :], gsp[0:npart, :], ghl[0:npart, 2, :], ALU.subtract, ["gsp", "ghl"], ["gres"])
        cp("dve", ghl[0:npart, 3, :], gres[0:npart, :], ["gres"], ["ghl"])
        for w in range(4):
            cp("pool", reps[0:npart, w, :, :], ghl[0:npart, w, :].unsqueeze(2).to_broadcast([npart, 4, 128]),
               ["ghl"], ["reps"])
        G = psf[:, 0, :].rearrange("p (h t) -> p h t", h=MH)[:, :, 0:T]
        Bp = psf[:, 1, :].rearrange("p (h t) -> p h t", h=MH)[:, :, 0:T]
        for h in range(MH):
            mm(G[:, h, :], reps[0:npart, 0, h, :], idb[0:npart, 0:T], True, False, ["reps", "idb"], ["f0"])
            mm(G[:, h, :], reps[0:npart, 1, h, :], idb[0:npart, 0:T], False, False, ["reps", "idb"], ["f0"])
            mm(G[:, h, :], reps[0:npart, 2, h, :], tri[0:npart, 0:T], False, False, ["reps", "tri"], ["f0"])
            mm(G[:, h, :], reps[0:npart, 3, h, :], tri[0:npart, 0:T], False, True, ["reps", "tri"], ["f0"])
            mm(Bp[:, h, :], reps[0:npart, 2, h, :], tri[0:npart, 0:T], True, False, ["reps", "tri"], ["f1"])
            mm(Bp[:, h, :], reps[0:npart, 3, h, :], tri[0:npart, 0:T], False, True, ["reps", "tri"], ["f1"])
        P.op("dve", lambda e: e.tensor_reduce(out=Rm[:, :], in_=G, axis=AX.X, op=ALU.max), ["f0"], ["Rm"])
        tt("dve", Rm[:, :], Rm[:, :], mst[:, :], ALU.max, ["Rm", mname], ["Rm"])
        Rb = Rm[:, :].unsqueeze(2).to_broadcast([128, MH, T])
        tt("dve", t1[:, :, 0:T], G, Rb, ALU.subtract, ["f0", "Rm"], ["t1"])
        act(a_bc[:, :, 0:T], t1[:, :, 0:T], AF.Exp, ["t1"], ["a_bc"])
        tt("dve", mtmp[:, :], mst[:, :], Rm[:, :], ALU.subtract, [mname, "Rm"], ["mtmp"])
        act(cm[:, :], mtmp[:, :], AF.Exp, ["mtmp"], ["cm"])
        tt("dve", t1[:, :, 0:T], Bp, Rb, ALU.subtract, ["f1", "Rm"], ["t1"])
        act(clampv[:, :, 0:T], t1[:, :, 0:T], AF.Exp, ["t1"], ["clampv"])
        tt("dve", mst[:, :], Rm[:, :], Bp[:, :, T - 1], ALU.subtract, ["Rm", "f1", "mtmp"], [mname])
        stt("dve", kpT[:, :, 0:T], ksilu[:, :, c0:c0 + T], float(MD ** -0.5), a_bc[:, :, 0:T], ALU.mult, ALU.mult,
            ["ksilu", "a_bc"], ["kpT"])
        for h in range(MH):
            tr(psb[0:T, 0, h * 128:(h + 1) * 128], kpT[:, h, 0:T], idb[:, :], ["kpT", "idb"], ["tb0"])
        cp("act", ktm[0:npart, :, :], psb[0:T, 0, :].rearrange("p (h d) -> p h d", h=MH), ["tb0"], ["ktm"])
        tt("dve", Sc_bf[:, :, :], state[:, :, :], cm[:, :].unsqueeze(2).to_broadcast([128, MH, MD + 1]), ALU.mult,
           [sname, "cm"], ["Sc_bf"])
        cp("pool", nrep[:, :, :], Sc_bf[:, :, MD:MD + 1].to_broadcast([128, MH, 128]), ["Sc_bf"], ["nrep"])
        STp = psf[:, 2, :].rearrange("p (h t) -> p h t", h=MH)[0:T, :, 0:T]
        for h in range(MH):
            mm(STp[:, h, :], kpT[:, h, 0:T], qmT[:, h, c0:c0 + T], True, True, ["kpT", "qmT"], ["f2"])
        tt("dve", STm[0:npart, :, 0:T], STp, tri[0:T, 0:T].unsqueeze(1).to_broadcast([T, MH, T]), ALU.mult,
           ["f2", "tri"], ["STm"])
        NTp = psf[:, 0, :].rearrange("p (h t) -> p h t", h=MH)[:, :, 0:T]
        DTp = psf[:, 1, :].rearrange("p (h t) -> p h t", h=MH)[:, :, 0:T]
        for h in range(MH):
            mm(NTp[:, h, :], vaug[0:npart, blk, h, 0:MD], STm[0:npart, h, 0:T], True, False, ["vaug", "STm"], ["f0"])
            mm(NTp[:, h, :], Sc_bf[:, h, 0:MD], qmT[:, h, c0:c0 + T], False, True, ["Sc_bf", "qmT"], ["f0"])
            mm(DTp[:, h, :], ones_bf[0:npart, :], STm[0:npart, h, 0:T], True, False, ["ones_bf", "STm"], ["f1"])
            mm(DTp[:, h, :], nrep[:, h, :], qmT[:, h, c0:c0 + T], False, True, ["nrep", "qmT"], ["f1"])
        tt("dve", t1[:, :, 0:T], DTp, clampv[:, :, 0:T], ALU.max, ["f1", "clampv"], ["t1"])
        stt("dve", t2[:, :, 0:T], DTp, -1.0, t1[:, :, 0:T], ALU.mult, ALU.max, ["f1", "t1"], ["t2"])
        recip(t2[:, :, 0:T], t2[:, :, 0:T], ["t2"], ["t2"])
        tt("dve", hm[:, :, 0:T], NTp, t2[:, :, 0:T], ALU.mult, ["f0", "t2"], ["a_bc"])
        act(sqh[:, :, 0:T], hm[:, :, 0:T], AF.Square, ["a_bc"], ["sqh"])
        HS = psf[:, 2, :].rearrange("p (h t) -> p h t", h=MH)[:, :, 0:T]
        for h in range(MH):
            mm(HS[:, h, :], ones_bf[:, :], sqh[:, h, 0:T], True, True, ["ones_bf", "sqh"], ["f2"])
        act(t1[:, :, 0:T], HS, AF.Sqrt, ["f2"], ["t1"], bias=EPS, scale=1.0 / MD)
        recip(t1[:, :, 0:T], t1[:, :, 0:T], ["t1"], ["t1"])
        tt("dve", t2[:, :, 0:T], hm[:, :, 0:T], t1[:, :, 0:T], ALU.mult, ["a_bc", "t1"], ["t2"])
        tt("dve", ymT[:, :, c0:c0 + T], t2[:, :, 0:T], omg[:, :, c0:c0 + T], ALU.mult, ["t2", "omg"], ["ymT"])
        for h in range(MH):
            bank = 0 if h < 2 else 1
            o = psf[:, bank, (h % 2) * 129:(h % 2) * 129 + 129]
            mm(o, ktm[0:npart, h, :], vaug[0:npart, blk, h, :], True, True, ["ktm", "vaug"], ["f%d" % bank])
        for h in range(MH):
            bank = 0 if h < 2 else 1
            o = psf[:, bank, (h % 2) * 129:(h % 2) * 129 + 129]
            stt("dve", state[:, h, :], state[:, h, :], cm[:, h:h + 1], o, ALU.mult, ALU.add,
                [sname, "cm", "f%d" % bank, "Sc_bf"], [sname])

    S2 = psf[:, 4:6, :].rearrange("p a (h q) -> p (a h) q", q=128)
    S4 = psf[:, 4:6, :].rearrange("p a (s q) -> p a s q", q=128)

    def hview(ap3):
        return ap3.rearrange("p (s a) q -> p a s q", a=2)

    def attn_scores(kb, lhs_fn, nk, q0, nq, table, trd):
        for h in range(NH):
            lhsT, lrd = lhs_fn(h)
            mm(S4[0:nk, h % 2, h // 2, 0:nq], lhsT, qaT[(h % 2) * 64:(h % 2) * 64 + 64, h // 2, q0:q0 + nq], True, True,
               lrd + ["qaT"], ["f%d" % (4 + h % 2)])
        tt("dve", hview(stmp[0:nk, :, 0:nq]), S4[0:nk, :, :, 0:nq], table, ALU.add, ["f4", "f5"] + trd, ["xin"])
        act(PT[0:nk, kb, :, 0:nq], stmp[0:nk, :, 0:nq], AF.Exp, ["xin"], ["PT"])

    OA = psf[:, 6, 0:260].rearrange("p (h d) -> p h d", d=65)
    OB = psf[:, 3, 0:260].rearrange("p (h d) -> p h d", d=65)

    def attn_out(nq, pv_list, q0):
        for h in range(NH):
            O = OA if h < 4 else OB
            bn = "f6" if h < 4 else "f3"
            for i, (kb, nk, rhs_fn) in enumerate(pv_list):
                rhs, rrd = rhs_fn(h)
                mm(O[0:nq, h % 4, :], PT[0:nk, kb, h, 0:nq], rhs, i == 0, i == len(pv_list) - 1, ["PT"] + rrd, [bn])
        for half, (O, bn) in enumerate(((OA, "f6"), (OB, "f3"))):
            recip(rden[0:nq, half * 4:half * 4 + 4], O[0:nq, :, DH], [bn], ["rden"])
            tt("dve", ya[0:nq, half * 4:half * 4 + 4, :], O[0:nq, :, 0:DH],
               rden[0:nq, half * 4:half * 4 + 4].unsqueeze(2).to_broadcast([nq, 4, DH]), ALU.mult,
               [bn, "rden"], ["ya"])
        yaf = ya[:, :, :].rearrange("p h d -> p (h d)")
        for f in range(4):
            tr(psb[:, 1, f * 128:f * 128 + nq], yaf[0:nq, f * 128:(f + 1) * 128], idb[0:nq, 0:nq], ["ya", "idb"], ["tb1"])
        cp("act", yaT[:, :, q0:q0 + nq], psb[:, 1, :].rearrange("p (f q) -> p f q", f=4)[:, :, 0:nq], ["tb1"], ["yaT"])

    def conv_evac(ps, bn, l, ftile, N, nseg, tail, tname):
        s = 0
        L = N // nseg
        cs = cst[:, s, 0:nseg * (L + 3)].rearrange("p (g t) -> p g t", g=nseg)
        cn = "cst%d" % s
        cp("pool", cs[:, :, 0:3], tail.rearrange("p r g -> p g r"), [tname], [cn])
        cp("act", cs[:, :, 3:3 + L], ps.rearrange("p (g t) -> p g t", g=nseg), [bn], [cn])
        ft_, fn_ = next_ftmp()
        acc = ft_[:, 0:N].rearrange("p (g t) -> p g t", g=nseg)
        wbase = 56
        eng = "dve"
        ts(eng, acc, cs[:, :, 0:L], vecT[:, l, wbase + ftile:wbase + ftile + 1], vecT[:, l, 88 + ftile:88 + ftile + 1],
           ALU.mult, ALU.add, [cn, "vecT"], [fn_])
        for j in range(1, 4):
            stt("dve", acc, cs[:, :, j:j + L], vecT[:, l, wbase + j * 8 + ftile:wbase + j * 8 + ftile + 1], acc,
                ALU.mult, ALU.add, [cn, "vecT", fn_], [fn_])
        cp("pool", tail.rearrange("p r g -> p g r"), cs[:, :, L:L + 3], [cn], [tname])
        if ftile < 4:
            act(qmT[:, ftile, 0:N], ft_[:, 0:N], AF.Silu, [fn_], ["qmT"])
        else:
            act(ksilu[:, ftile - 4, 0:N], ft_[:, 0:N], AF.Silu, [fn_], ["ksilu"])

    def layer_setup(l):
        stt("dve", gsc1[:, :, :], modT[:, l, 8:16, :], 1.0, vecT[:, l, 48:56].unsqueeze(2).to_broadcast([128, 8, 5]),
            ALU.add, ALU.mult, ["modT", "vecT"], ["gsc"])
        stt("dve", gsc2[:, :, :], modT[:, l, 32:40, :], 1.0, vecT[:, l, 100:108].unsqueeze(2).to_broadcast([128, 8, 5]),
            ALU.add, ALU.mult, ["modT", "vecT"], ["gsc"])
        dma(bif[:, :], b_if.ap()[l].partition_broadcast(128), [], ["bif"], "bif")
        dma(Esb[:, 64:384], rel_bias.ap()[l], [], ["xin"], "xin")
        cp("dve", Esb[:, 0:64], Esb[:, 64:65].to_broadcast([8, 64]), ["xin"], ["xin"])
        cp("dve", Esb[:, 384:512], Esb[:, 383:384].to_broadcast([8, 128]), ["xin"], ["xin"])
        dma(Escr.ap()[l], Esb[:, :], ["xin"], ["Escr"], "Escr")
        ts("dve", dg[:, :], idf[0:8, 0:8], Esb[:, 511:512], None, ALU.mult, None, ["idf", "xin"], ["dg"])
        mm(psf[:, 0, 0:8], ones8[:, :], dg[:, :], True, True, ["ones8", "dg"], ["f0"])
        cp("dve", chead[:, :], psf[:, 0, 0:8], ["f0"], ["chead"])
        for kb in (2, 3, 4):
            src = bass.AP(Escr, l * NH * 512 + (4 - kb) * 128, [[1, 128], [512, NH], [1, 128]])
            dma(stmp[:, :, :], src, ["Escr"], ["xin"], "hank")
            sflat = stmp[:, :, :].rearrange("p h q -> p (h q)")
            for half in range(2):
                mm(psf[:, 4 + half, :], Jf[:, :], sflat[:, half * 512:(half + 1) * 512], True, True,
                   ["Jf", "xin"], ["f%d" % (4 + half)])
            cp("dve", btab[:, kb - 2, :, :], S2, ["f4", "f5"], ["btab"])
        memset("pool", state_p[:, :, :], 0.0, ["state_p"])
        memset("pool", m_p[:, :], 0.0, ["m_p"])
        memset("pool", ctail[:, :, :], 0.0, ["ctail"])

    def bias_table(kb, nk, nq):
        if kb < 2:
            return chead[0:nk, :].rearrange("p (s a) -> p a s", a=2).unsqueeze(3).to_broadcast([nk, 2, 4, nq]), ["chead"]
        return hview(btab[0:nk, kb - 2, :, 0:nq]), ["btab"]

    def proj_phase(l, N, groups, is_prompt, tile, last):
        for pi in range(8):
            c0, w = IN_PANELS[pi]
            panel, pname = load_panel("in", l, pi, 8, w)
            if pi in (0, 1, 3, 4, 6):
                for ct in range(4):
                    ps, bn = fm_proj(panel, pname, KT, ct, hT, "hT", N)
                    if pi == 0:
                        act(qaT[:, ct, 0:N], ps, AF.Identity, [bn], ["qaT"], scale=float(DH ** -0.5))
                    elif pi == 1:
                        if is_prompt:
                            pos0 = (tile % 2) * NT
                            cp("dve", kaT[:, ct, pos0:pos0 + N], ps, [bn], ["kaT"])
                        else:
                            cp("dve", kaT[:, ct, 0:N], ps, [bn], ["kaT"])
                    elif pi == 3:
                        if is_prompt:
                            conv_evac(ps, bn, l, ct, N, 1, ctail[:, :, ct:ct + 1], "ctail")
                        else:
                            conv_evac(ps, bn, l, ct, N, NSS, ctail_s[:, :, ct, :], "ctail_s")
                    elif pi == 4:
                        if is_prompt:
                            conv_evac(ps, bn, l, 4 + ct, N, 1, ctail[:, :, 4 + ct:5 + ct], "ctail")
                        else:
                            conv_evac(ps, bn, l, 4 + ct, N, NSS, ctail_s[:, :, 4 + ct, :], "ctail_s")
                    else:
                        ft_, fn_ = next_ftmp()
                        act(ft_[:, 0:N], ps, AF.Sigmoid, [bn], [fn_])
                        ts("dve", omg[:, ct, 0:N], ft_[:, 0:N], vecT[:, l, 96 + ct:97 + ct], None, ALU.mult, None,
                           [fn_, "vecT"], ["omg"])
                if pi == 1 and (last or not is_prompt):
                    for gi, (g0, gn) in enumerate(groups):
                        ps, bn = tm_proj(panel, pname, g0, gn, 512)
                        cp("dve", xin[0:gn, 0:512], ps, [bn], ["xin"])
                        dst = nkp.ap()[l, g0:g0 + gn, :] if is_prompt else nks.ap()[l, gi]
                        dma(dst, xin[0:gn, 0:512], ["xin"], [], "st_k")
            else:
                for gi, (g0, gn) in enumerate(groups):
                    ps, bn = tm_proj(panel, pname, g0, gn, w)
                    if pi == 2:
                        if is_prompt:
                            rb_ = (tile % 2) * 4 + gi
                        else:
                            rb_ = gi
                        cp("dve", Vring[0:gn, rb_, :, 0:DH], ps.rearrange("p (h d) -> p h d", d=DH), [bn], ["Vring"])
                        if last or not is_prompt:
                            cp("dve", xin[0:gn, 512:1024], ps, [bn], ["xin"])
                            dst = nvp.ap()[l, g0:g0 + gn, :] if is_prompt else nvs.ap()[l, gi]
                            dma(dst, xin[0:gn, 512:1024], ["xin"], [], "st_v")
                    elif pi == 5:
                        cp("act", vaug[0:gn, gi, :, 0:MD], ps.rearrange("p (h d) -> p h d", d=MD), [bn], ["vaug"])
                    else:
                        cp("dve", gate_tm[0:gn, gi, :], ps, [bn], ["gate_tm"])

    def merge_ffn_phase(l, N, xbuf, xname, segs):
        for half in range(2):
            pa, pan = load_panel("bra", l, half, 4, 512)
            pg, pgn = load_panel("in", l, 8 + half, 8, 512)
            for c in range(4):
                ct = half * 4 + c
                ba, ban = fm_proj(pa, pan, 4, c, yaT, "yaT", N)
                ga, gan = fm_proj(pg, pgn, KT, c, hT, "hT", N)
                f1, f1n = next_ftmp()
                act(f1[:, 0:N], ga, AF.Sigmoid, [gan], [f1n])
                tt("dve", f1[:, 0:N], ba, f1[:, 0:N], ALU.mult, [ban, f1n], [f1n])
                msv, msn = mstash(ct)
                cp("act", msv[:, 0:N], f1[:, 0:N], [f1n], [msn])
        P.mark("mf_a")
        for half in range(2):
            pm, pmn = load_panel("brm", l, half, 4, 512)
            pg, pgn = load_panel("in", l, 10 + half, 8, 512)
            for c in range(4):
                ct = half * 4 + c
                bm, bmn = fm_proj(pm, pmn, 4, c, ymT, "ymT", N)
                gm, gmn = fm_proj(pg, pgn, KT, c, hT, "hT", N)
                f1, f1n = next_ftmp()
                act(f1[:, 0:N], gm, AF.Sigmoid, [gmn], [f1n])
                tt("dve", f1[:, 0:N], bm, f1[:, 0:N], ALU.mult, [bmn, f1n], [f1n])
                msv, msn = mstash(ct)
                mgv, mgn = mergedv(ct)
                tt("dve", mgv[:, 0:N], f1[:, 0:N], msv[:, 0:N], ALU.add, [f1n, msn], [mgn])
        P.mark("mf_b")
        for half in range(2):
            po, pon = load_panel("out", l, half, 8, 512)
            for c in range(4):
                ct = half * 4 + c
                o, on = fm_proj(po, pon, KT, c, mergedv, None, N)
                for (c0, n, r) in segs:
                    stt("dve", xbuf[:, ct, c0:c0 + n], o[:, c0:c0 + n], modT[:, l, 16 + ct, r:r + 1],
                        xbuf[:, ct, c0:c0 + n], ALU.mult, ALU.add, [on, "modT", xname], [xname])
        P.mark("mf_out")
        norm_mod(xbuf, xname, N, segs, gsc2, 24, l)
        P.mark("mf_norm")
        for f in range(FT):
            pgu, pgun = load_panel("gu", l, f, 8, 256)
            g_, gn_ = fm_proj(pgu, pgun, KT, 0, hT, "hT", N)
            u_, un_ = fm_proj(pgu, pgun, KT, 1, hT, "hT", N)
            f1, f1n = next_ftmp()
            act(f1[:, 0:N], g_, AF.Silu, [gn_], [f1n])
            tt("dve", actT[:, f, 0:N], u_, f1[:, 0:N], ALU.mult, [un_, f1n], ["actT"])
        P.mark("mf_gu")
        for ct in range(8):
            pd, pdn = load_panel("dn", l, ct, FT, 128)
            o, on = fm_proj(pd, pdn, FT, 0, actT, "actT", N)
            for (c0, n, r) in segs:
                stt("dve", xbuf[:, ct, c0:c0 + n], o[:, c0:c0 + n], modT[:, l, 40 + ct, r:r + 1],
                    xbuf[:, ct, c0:c0 + n], ALU.mult, ALU.add, [on, "modT", xname], [xname])


    def final_out(xbuf, xname, N, ydst_fn):
        rms_stats([(xbuf[:, kt, 0:N], [xname]) for kt in range(KT)], N, 1.0 / D)
        for blk in range(N // 128):
            c0 = blk * 128
            for kt in range(KT):
                ft_, fn_ = next_ftmp()
                stt("dve", ft_[:, 0:128], xbuf[:, kt, c0:c0 + 128], vecT[:, 1, 108 + kt:109 + kt], rstd[:, c0:c0 + 128],
                    ALU.mult, ALU.mult, [xname, "vecT", "rstd"], [fn_])
                b = kt // 4
                tr(psf[:, b, (kt % 4) * 128:(kt % 4) * 128 + 128], ft_[:, 0:128], idf[:, :], [fn_, "idf"], ["f%d" % b])
            cp("act", xin[:, 0:512], psf[:, 0, :], ["f0"], ["xin"])
            cp("dve", xin[:, 512:1024], psf[:, 1, :], ["f1"], ["xin"])
            dma(ydst_fn(blk), xin[:, :], ["xin"], [], "st_y")

    def prompt_tile(l, j):
        last = (j == NTILES - 1)
        t0 = j * NT
        if l == 0:
            for blk in range(4):
                dma(xin[:, :], xp.ap()[t0 + blk * 128:t0 + (blk + 1) * 128, :], [], ["xin"], "ld_x")
                for kt in range(KT):
                    b = kt // 4
                    tr(psf[:, b, (kt % 4) * 128:(kt % 4) * 128 + 128], xin[:, kt * 128:(kt + 1) * 128], idf[:, :],
                       ["xin", "idf"], ["f%d" % b])
                cp("act", xT[:, 0:4, blk * 128:(blk + 1) * 128], psf[:, 0, :].rearrange("p (k t) -> p k t", k=4),
                   ["f0"], ["xT"])
                cp("dve", xT[:, 4:8, blk * 128:(blk + 1) * 128], psf[:, 1, :].rearrange("p (k t) -> p k t", k=4),
                   ["f1"], ["xT"])
        else:
            dma(xT[:, :, :], xscr.ap()[:, :, t0:t0 + NT].rearrange("k p t -> p k t"), ["xscr%d" % j], ["xT"], "ld_x")
        segs = [(0, NT, 0)]
        P.mark("p%d_%d_load" % (l, j))
        norm_mod(xT, "xT", NT, segs, gsc1, 0, l)
        P.mark("p%d_%d_norm" % (l, j))
        groups = [(b * 128, 128) for b in range(4)]
        proj_phase(l, NT, groups, True, j, last)
        P.mark("p%d_%d_proj" % (l, j))
        for pq in range(4):
            gb = j * 4 + pq
            pv = []
            for kb in range(5):
                gkb = gb - 4 + kb
                if gkb < 0:
                    continue
                rbk = gkb % 8
                table, trd = bias_table(kb, 128, 128)

                def lhs_fn(h, rbk=rbk):
                    return kaT[(h % 2) * 64:(h % 2) * 64 + 64, h // 2, rbk * 128:(rbk + 1) * 128], ["kaT"]
                attn_scores(kb, lhs_fn, 128, pq * 128, 128, table, trd)
                if kb == 0:
                    memset("pool", PT[0:64, 0, :, 64:128], 0.0, ["PT"])
                if kb == 4:
                    memset("pool", PT[64:128, 4, :, 0:64], 0.0, ["PT"])

                def rhs_fn(h, rbk=rbk):
                    return Vring[:, rbk, h, :], ["Vring"]
                pv.append((kb, 128, rhs_fn))
            attn_out(128, pv, pq * 128)
            P.mark("p%d_%d_attn%d" % (l, j, pq))
            mlstm_chunk(128, pq * 128, pq, state_p, "state_p", m_p, "m_p")
            P.mark("p%d_%d_mlstm%d" % (l, j, pq))
        merge_ffn_phase(l, NT, xT, "xT", segs)
        P.mark("p%d_%d_ffn" % (l, j))
        if l == 0:
            dma(xscr.ap()[:, :, t0:t0 + NT].rearrange("k p t -> p k t"), xT[:, :, :], ["xT"], ["xscr%d" % j], "st_x")
        else:
            final_out(xT, "xT", NT, lambda blk: yp.ap()[t0 + blk * 128:t0 + (blk + 1) * 128, :])
        if last:
            tr(psf[0:24, 2, 0:128], ctail[:, :, :].rearrange("p r f -> p (r f)"), idf[:, :], ["ctail", "idf"], ["f2"])
            cp("dve", xin[0:24, 0:128], psf[0:24, 2, 0:128], ["f2"], ["xin"])
            dma(ncp.ap()[l], xin[0:24, 0:128], ["xin"], [], "st_c")
            dma(nCp.ap()[l].rearrange("h k v -> k h v"), state_p[:, :, 0:MD], ["state_p"], [], "st_C")
            tr(psf[0:4, 2, 128:256], state_p[:, :, MD], idf[:, :], ["state_p", "idf"], ["f2"])
            cp("dve", xin[0:4, 128:256], psf[0:4, 2, 128:256], ["f2"], ["xin"])
            dma(nnp.ap()[l], xin[0:4, 128:256], ["xin"], [], "st_n")
            dma(nmp.ap()[l:l + 1, :], m_p[0:1, :], ["m_p"], [], "st_m")

    def sample_tile(l):
        N = NSS * TS
        if l == 0:
            dma(xin[:, :], xs.ap(), [], ["xin"], "ld_x")
            for kt in range(KT):
                b = kt // 4
                tr(psf[:, b, (kt % 4) * 128:(kt % 4) * 128 + 128], xin[:, kt * 128:(kt + 1) * 128], idf[:, :],
                   ["xin", "idf"], ["f%d" % b])
            cp("act", xsT[:, 0:4, :], psf[:, 0, :].rearrange("p (k t) -> p k t", k=4), ["f0"], ["xsT"])
            cp("dve", xsT[:, 4:8, :], psf[:, 1, :].rearrange("p (k t) -> p k t", k=4), ["f1"], ["xsT"])
        segs = [(s * TS, TS, 1 + s) for s in range(NSS)]
        for s in range(NSS):
            dma(xin[0:24, 0:128], sconv.ap()[l, s], [], ["xin"], "ld_sc")
            tr(psf[:, 2, 0:24], xin[0:24, 0:128], idf[0:24, 0:24], ["xin", "idf"], ["f2"])
            cp("dve", ctail_s[:, :, :, s], psf[:, 2, 0:24].rearrange("p (r f) -> p r f", r=3), ["f2"], ["ctail_s"])
        norm_mod(xsT, "xsT", N, segs, gsc1, 0, l)
        groups = [(s * TS, TS) for s in range(NSS)]
        proj_phase(l, N, groups, False, 0, False)
        for s in range(NSS):
            q0 = s * TS
            memset("pool", cvb[:, :, :, DH:DH + 1], 1.0, ["actT"])
            dma(ckb, ck.ap()[l, s].rearrange("(kb p) f -> p kb f", p=128), [], ["actT"], "ld_ck", eng="pool")
            for kb in range(4):
                dma(cvb[:, kb, :, 0:DH], cv.ap()[l, s][kb * 128:(kb + 1) * 128, :].rearrange("p (h d) -> p h d", d=DH),
                    [], ["actT"], "ld_cv", eng="pool")
            for kb in range(4):
                for f in range(4):
                    tr(psb[:, kb % 2, f * 128:(f + 1) * 128], ckb[:, kb, f * 128:(f + 1) * 128], idb[:, :],
                       ["actT", "idb"], ["tb%d" % (kb % 2)])
                cp("act" if kb % 2 == 0 else "dve", ckT[:, :, kb * 128:(kb + 1) * 128],
                   psb[:, kb % 2, :].rearrange("p (f k) -> p f k", f=4), ["tb%d" % (kb % 2)], ["actT"])
            pv = []
            for kb in range(4):
                table, trd = bias_table(kb, 128, TS)

                def lhs_fn(h, kb=kb):
                    return ckT[(h % 2) * 64:(h % 2) * 64 + 64, h // 2, kb * 128:(kb + 1) * 128], ["actT"]
                attn_scores(kb, lhs_fn, 128, q0, TS, table, trd)

                def rhs_fn(h, kb=kb):
                    return cvb[:, kb, h, :], ["actT"]
                pv.append((kb, 128, rhs_fn))
            table, trd = bias_table(4, TS, TS)

            def lhs_new(h, q0=q0):
                return kaT[(h % 2) * 64:(h % 2) * 64 + 64, h // 2, q0:q0 + TS], ["kaT"]
            attn_scores(4, lhs_new, TS, q0, TS, table, trd)

            def rhs_new(h, s=s):
                return Vring[0:TS, s, h, :], ["Vring"]
            pv.append((4, TS, rhs_new))
            attn_out(TS, pv, q0)
            dma(state_s[:, :, 0:MD], sC.ap()[l, s].rearrange("h k v -> k h v"), [], ["state_s"], "ld_sC")
            dma(xin[0:4, 128:256], sn.ap()[l, s], [], ["xin"], "ld_sn")
            tr(psf[:, 2, 32:36], xin[0:4, 128:256], idf[0:4, 0:4], ["xin", "idf"], ["f2"])
            cp("dve", state_s[:, :, MD], psf[:, 2, 32:36], ["f2"], ["state_s"])
            dma(m_s[:, :], sm.ap()[l, s * MH:(s + 1) * MH].partition_broadcast(128), [], ["m_s"], "ld_sm")
            mlstm_chunk(TS, q0, s, state_s, "state_s", m_s, "m_s")
            dma(nCs.ap()[l, s].rearrange("h k v -> k h v"), state_s[:, :, 0:MD], ["state_s"], [], "st_C")
            tr(psf[0:4, 2, 128:256], state_s[:, :, MD], idf[:, :], ["state_s", "idf"], ["f2"])
            cp("dve", xin[0:4, 256:384], psf[0:4, 2, 128:256], ["f2"], ["xin"])
            dma(nns.ap()[l, s], xin[0:4, 256:384], ["xin"], [], "st_n")
            dma(nms.ap()[l, s:s + 1, :], m_s[0:1, :], ["m_s"], [], "st_m")
        for s in range(NSS):
            cp("dve", stmp[:, 0, 0:24].rearrange("p (r f) -> p r f", r=3), ctail_s[:, :, :, s], ["ctail_s"], ["xin"])
            tr(psf[0:24, 2, 0:128], stmp[:, 0, 0:24], idf[:, :], ["xin", "idf"], ["f2"])
            cp("dve", xin[0:24, 384:512], psf[0:24, 2, 0:128], ["f2"], ["xin"])
            dma(ncs.ap()[l, s], xin[0:24, 384:512], ["xin"], [], "st_c")
        merge_ffn_phase(l, N, xsT, "xsT", segs)
        if l == 1:
            final_out(xsT, "xsT", N, lambda blk: ys.ap())

    P.mark("pre")
    for l in range(2):
        layer_setup(l)
        P.mark("setup%d" % l)
        for j in range(NTILES):
            prompt_tile(l, j)
            P.mark("ptile%d_%d" % (l, j))
        sample_tile(l)
        P.mark("stile%d" % l)
    import os
    lim = os.environ.get("KLIMIT")
    if lim:
        d = {}
        for k_, v_ in P.marks:
            d.setdefault(k_, v_)
        P.limit = d[lim] if lim in d else int(lim)
    print("marks", P.marks)
    P.build()
    return nc, P


_CACHE = {}


def kernel(x_prompt, x_sample, cache_k, cache_v, state_conv, state_C, state_n, state_m,
           c_prompt, c_sample, w_ada, b_ada, g_mix, w_in, b_if, conv_w, conv_b, rel_bias,
           mh_gain, w_br_att, w_br_mlstm, w_out, g_ffn, w_gate_up, w_down, g_final):
    f = lambda a: np.ascontiguousarray(np.asarray(a, dtype=np.float32))
    x_prompt = f(x_prompt); x_sample = f(x_sample)
    B, SEQ, _ = x_prompt.shape
    NCORE = 8
    if SEQ not in _CACHE:
        _CACHE[SEQ] = build_program(SEQ)[0]
    nc = _CACHE[SEQ]
    vecs = np.zeros((2, 116, 128), np.float32)
    for l in range(2):
        vecs[l, 0:48] = f(b_ada)[l].reshape(48, 128)
        vecs[l, 48:56] = f(g_mix)[l].reshape(8, 128)
        vecs[l, 56:88] = f(conv_w)[l].reshape(32, 128)
        vecs[l, 88:96] = f(conv_b)[l].reshape(8, 128)
        vecs[l, 96:100] = f(mh_gain)[l].reshape(4, 128)
        vecs[l, 100:108] = f(g_ffn)[l].reshape(8, 128)
        vecs[l, 108:116] = f(g_final).reshape(8, 128)
    shared = dict(vecs=vecs, b_if=f(b_if), rel_bias=f(rel_bias), w_ada=f(w_ada), w_in=f(w_in),
                  w_bra=f(w_br_att), w_brm=f(w_br_mlstm), w_out=f(w_out), w_gu=f(w_gate_up), w_dn=f(w_down))
    cache_k = f(cache_k); cache_v = f(cache_v); state_conv = f(state_conv)
    state_C = f(state_C); state_n = f(state_n); state_m = f(state_m)
    c_prompt = f(c_prompt); c_sample = f(c_sample)
    in_maps = []
    for c in range(NCORE):
        b = c % B
        s0, s1 = c * NSS, (c + 1) * NSS
        m = dict(shared)
        m["xp"] = x_prompt[b]
        m["xs"] = x_sample[s0:s1].reshape(NSS * TS, D)
        m["c5"] = np.ascontiguousarray(np.concatenate([c_prompt[b:b + 1], c_sample[s0:s1]], 0))
        m["ck"] = np.ascontiguousarray(cache_k[:, s0:s1].reshape(2, NSS, 512, 512))
        m["cv"] = np.ascontiguousarray(cache_v[:, s0:s1].reshape(2, NSS, 512, 512))
        m["sconv"] = np.ascontiguousarray(state_conv[:, s0:s1].reshape(2, NSS, 24, 128))
        m["sC"] = np.ascontiguousarray(state_C[:, s0:s1])
        m["sn"] = np.ascontiguousarray(state_n[:, s0:s1])
        m["sm"] = np.ascontiguousarray(state_m[:, s0:s1].reshape(2, NSS * MH))
        in_maps.append(m)
    res = run_bass_kernel_spmd(nc, in_maps, core_ids=list(range(NCORE)))
    R = res.results
    keep = 512
    y_prompt = np.stack([R[b]["yp"] for b in range(B)], 0)
    y_sample = np.concatenate([R[c]["ys"].reshape(NSS, TS, D) for c in range(NCORE)], 0)
    nkp = np.stack([R[b]["nkp"] for b in range(B)], 1).reshape(2, B, keep, NH, DH)
    nvp = np.stack([R[b]["nvp"] for b in range(B)], 1).reshape(2, B, keep, NH, DH)
    ncp = np.stack([R[b]["ncp"] for b in range(B)], 1).reshape(2, B, 3, 1024)
    nCp = np.stack([R[b]["nCp"] for b in range(B)], 1)
    nnp_ = np.stack([R[b]["nnp"] for b in range(B)], 1)
    nmp = np.stack([R[b]["nmp"] for b in range(B)], 1)
    nks = np.concatenate([R[c]["nks"] for c in range(NCORE)], 1).reshape(2, NCORE * NSS, TS, NH, DH)
    nvs = np.concatenate([R[c]["nvs"] for c in range(NCORE)], 1).reshape(2, NCORE * NSS, TS, NH, DH)
    ncs = np.concatenate([R[c]["ncs"] for c in range(NCORE)], 1).reshape(2, NCORE * NSS, 3, 1024)
    nCs = np.concatenate([R[c]["nCs"] for c in range(NCORE)], 1)
    nns = np.concatenate([R[c]["nns"] for c in range(NCORE)], 1)
    nms = np.concatenate([R[c]["nms"] for c in range(NCORE)], 1)
    outs = (y_prompt, y_sample, nkp, nvp, ncp, nCp, nnp_, nmp, nks, nvs, ncs, nCs, nns, nms)
    return tuple(np.ascontiguousarray(o, dtype=np.float32) for o in outs)
```
